# Optimizing a Trainium2 kernel written in Bass

```python
import jax, jax.numpy as jnp
from jax import lax
import numpy as np

D_MODEL = 1024
BATCH = 8
SEQ = 2048
DEPTH = 1
DEC_BATCH = 128
DEC_SEQ = 8
PAST_LEN = 16384
PAGE_SIZE = 128

POOL_WIDTH = D_MODEL
POOL_WINDOWS = (2, 4, 8, 16)
N_POOL_GROUPS = len(POOL_WINDOWS)
POOL_GROUP = POOL_WIDTH // N_POOL_GROUPS
POOL_HIST = max(POOL_WINDOWS) - 1
SSD_WIDTH = 2 * D_MODEL
SSD_HEAD_DIM = 64
SSD_HEADS = SSD_WIDTH // SSD_HEAD_DIM
SSD_GROUPS = 4
SSD_HPG = SSD_HEADS // SSD_GROUPS
SSD_STATE = 128
CONV_WIDTH = 4
CONV_DIM = SSD_WIDTH + 2 * SSD_GROUPS * SSD_STATE
SSD_CHUNK = 128
MEM_LEN = 256
ATT_HEADS = 4
ATT_WIDTH = D_MODEL
ATT_HEAD_DIM = ATT_WIDTH // ATT_HEADS
N_BRANCH = 3
EPS = 1e-6
IN_SPLITS = (POOL_WIDTH, POOL_WIDTH, SSD_WIDTH, CONV_DIM, SSD_HEADS, ATT_WIDTH, ATT_WIDTH, N_BRANCH * D_MODEL)
IN_COLS = sum(IN_SPLITS)

kernel_name = 'hybrid_pool_ssd_memattn_decoder_step'


def _rmsnorm(x, w):
    xf = x.astype(jnp.float32)
    y = xf * lax.rsqrt(jnp.mean(xf * xf, axis=-1, keepdims=True) + EPS)
    return (y * w.astype(jnp.float32)).astype(x.dtype)


def _split_points():
    pts, acc = [], 0
    for s in IN_SPLITS[:-1]:
        acc += s
        pts.append(acc)
    return pts


def _pool_mixer(ext, pos0, w_grp, scale):
    b, T, C = ext.shape
    L = T - POOL_HIST
    cs = jnp.cumsum(ext.astype(jnp.float32), axis=1)
    cs0 = jnp.concatenate([jnp.zeros((b, 1, C), jnp.float32), cs], axis=1)
    pos = pos0 + jnp.arange(L)
    outs = []
    for g, w in enumerate(POOL_WINDOWS):
        lo, hi = g * POOL_GROUP, (g + 1) * POOL_GROUP
        win = cs0[:, POOL_HIST + 1:POOL_HIST + 1 + L, lo:hi] - cs0[:, POOL_HIST + 1 - w:POOL_HIST + 1 - w + L, lo:hi]
        cnt = jnp.minimum(w, pos + 1).astype(jnp.float32)
        outs.append(win / cnt[None, :, None])
    pooled = jnp.concatenate(outs, axis=-1)
    u = ext[:, POOL_HIST:]
    d = (pooled - u.astype(jnp.float32)).astype(ext.dtype).reshape(b, L, N_POOL_GROUPS, POOL_GROUP)
    y = jnp.einsum('blgc,gcd->blgd', d, w_grp).reshape(b, L, C)
    return y * scale


def _causal_dwconv(ext, w, bias):
    L = ext.shape[1] - (CONV_WIDTH - 1)
    out = bias
    for k in range(CONV_WIDTH):
        out = out + ext[:, k:k + L] * w[k]
    return out


def _ssd_scan(x, dt, A, B, C, h0):
    b, L, G, R, P = x.shape
    N = B.shape[-1]
    Q = SSD_CHUNK if L % SSD_CHUNK == 0 else L
    nc = L // Q
    xf = x.astype(jnp.float32).reshape(b, nc, Q, G, R, P)
    dtc = dt.reshape(b, nc, Q, G, R)
    Bc = B.astype(jnp.float32).reshape(b, nc, Q, G, N)
    Cc = C.astype(jnp.float32).reshape(b, nc, Q, G, N)
    acs = jnp.cumsum(dtc * A, axis=2)
    acs_t = jnp.moveaxis(acs, 2, -1)
    seg = acs_t[..., :, None] - acs_t[..., None, :]
    mask = jnp.tril(jnp.ones((Q, Q), dtype=bool))
    Lmat = jnp.where(mask, jnp.exp(jnp.where(mask, seg, 0.0)), 0.0)
    xdt = xf * dtc[..., None]
    CB = jnp.einsum('bcqgn,bckgn->bcgqk', Cc, Bc)
    y_diag = jnp.einsum('bcgqk,bcgrqk,bckgrp->bcqgrp', CB, Lmat, xdt)
    decay_out = jnp.exp(acs[:, :, -1:] - acs)
    states = jnp.einsum('bckgn,bckgr,bckgrp->bcgrpn', Bc, decay_out, xdt)
    chunk_decay = jnp.exp(acs[:, :, -1])

    def step(h, inp):
        cd, st = inp
        return h * cd[..., None, None] + st, h

    h_final, h_prev = lax.scan(step, h0.astype(jnp.float32),
                               (jnp.moveaxis(chunk_decay, 1, 0), jnp.moveaxis(states, 1, 0)))
    h_prev = jnp.moveaxis(h_prev, 0, 1)
    y_off = jnp.einsum('bcqgn,bcgrpn,bcqgr->bcqgrp', Cc, h_prev, jnp.exp(acs))
    y = (y_diag + y_off).reshape(b, L, G, R, P)
    return y, h_final.astype(h0.dtype)


def _mem_kv(mem, mem_norm_w, w_mem_k, w_mem_v):
    b, M, _ = mem.shape
    mh = _rmsnorm(mem, mem_norm_w)
    k = jnp.einsum('bmd,de->bme', mh, w_mem_k).reshape(b, M, ATT_HEADS, ATT_HEAD_DIM)
    v = jnp.einsum('bmd,de->bme', mh, w_mem_v).reshape(b, M, ATT_HEADS, ATT_HEAD_DIM)
    return k, v


def _mem_attention(q, k, v):
    b, L, _ = q.shape
    qh = q.reshape(b, L, ATT_HEADS, ATT_HEAD_DIM)
    s = jnp.einsum('blhd,bmhd->bhlm', qh, k).astype(jnp.float32) * (ATT_HEAD_DIM ** -0.5)
    p = jax.nn.softmax(s, axis=-1).astype(v.dtype)
    return jnp.einsum('bhlm,bmhd->blhd', p, v).reshape(b, L, ATT_WIDTH)


def _layer(x, hist_pool, hist_conv, h0, mem_k, mem_v, pos0, norm_w, w_in, w_pool_grp, pool_scale,
           conv_w, conv_b, dt_bias, a_log, d_skip, ssd_norm_w, w_pool_out, w_ssd_out, w_att_out, w_out):
    b, L, _ = x.shape
    h = _rmsnorm(x, norm_w)
    proj = jnp.einsum('bld,de->ble', h, w_in)
    u_pool, z_pool, z_ssd, xbc, dt_raw, q, z_att, gate_logits = jnp.split(proj, _split_points(), axis=-1)
    pool_ext = jnp.concatenate([hist_pool.astype(x.dtype), u_pool], axis=1)
    y_pool = _pool_mixer(pool_ext, pos0, w_pool_grp, pool_scale) * jax.nn.silu(z_pool)
    conv_ext = jnp.concatenate([hist_conv.astype(x.dtype), xbc], axis=1)
    xbc_c = jax.nn.silu(_causal_dwconv(conv_ext, conv_w, conv_b))
    xs, Bm, Cm = jnp.split(xbc_c, [SSD_WIDTH, SSD_WIDTH + SSD_GROUPS * SSD_STATE], axis=-1)
    xs = xs.reshape(b, L, SSD_GROUPS, SSD_HPG, SSD_HEAD_DIM)
    Bm = Bm.reshape(b, L, SSD_GROUPS, SSD_STATE)
    Cm = Cm.reshape(b, L, SSD_GROUPS, SSD_STATE)
    dt = jax.nn.softplus(dt_raw.astype(jnp.float32) + dt_bias.astype(jnp.float32)).reshape(b, L, SSD_GROUPS, SSD_HPG)
    A = -jnp.exp(a_log.astype(jnp.float32)).reshape(SSD_GROUPS, SSD_HPG)
    y_ssm, h_new = _ssd_scan(xs, dt, A, Bm, Cm, h0)
    y_ssm = y_ssm + d_skip.astype(jnp.float32).reshape(SSD_GROUPS, SSD_HPG, 1) * xs.astype(jnp.float32)
    y_ssm = y_ssm.reshape(b, L, SSD_WIDTH).astype(x.dtype)
    y_ssd = _rmsnorm(y_ssm * jax.nn.silu(z_ssd), ssd_norm_w)
    y_att = _mem_attention(q, mem_k, mem_v) * jax.nn.silu(z_att)
    gates = jax.nn.sigmoid(gate_logits.astype(jnp.float32)).astype(x.dtype).reshape(b, L, N_BRANCH, D_MODEL)
    merged = (gates[:, :, 0] * (y_pool @ w_pool_out)
              + gates[:, :, 1] * (y_ssd @ w_ssd_out)
              + gates[:, :, 2] * (y_att @ w_att_out))
    x_out = x + merged @ w_out
    return x_out, pool_ext[:, -POOL_HIST:], conv_ext[:, -(CONV_WIDTH - 1):], h_new


def setup_inputs(seed: int = 0) -> dict:
    key = jax.random.key(seed)
    ks = jax.random.split(key, 32)
    f32 = jnp.float32
    nrm = lambda k, shape, s: jax.random.normal(k, shape, f32) * s
    dt_init = jnp.exp(jax.random.uniform(ks[14], (DEPTH, SSD_HEADS), f32, np.log(1e-3), np.log(1e-1)))
    return {
        'x_prompt': nrm(ks[0], (BATCH, SEQ, D_MODEL), 1.0),
        'x_sample': nrm(ks[1], (DEC_BATCH, DEC_SEQ, D_MODEL), 1.0),
        'mem_prompt': nrm(ks[2], (BATCH, MEM_LEN, D_MODEL), 1.0),
        'state_pool': nrm(ks[3], (DEPTH, DEC_BATCH, POOL_HIST, POOL_WIDTH), 1.0),
        'state_conv': nrm(ks[4], (DEPTH, DEC_BATCH, CONV_WIDTH - 1, CONV_DIM), 1.0),
        'state_ssm': nrm(ks[5], (DEPTH, DEC_BATCH, SSD_GROUPS, SSD_HPG, SSD_HEAD_DIM, SSD_STATE), 0.1),
        'cache_mem_k': nrm(ks[6], (DEPTH, DEC_BATCH, MEM_LEN, ATT_HEADS, ATT_HEAD_DIM), 1.0),
        'cache_mem_v': nrm(ks[7], (DEPTH, DEC_BATCH, MEM_LEN, ATT_HEADS, ATT_HEAD_DIM), 1.0),
        'norm_w': 1.0 + nrm(ks[8], (DEPTH, D_MODEL), 0.02),
        'w_in': nrm(ks[9], (DEPTH, D_MODEL, IN_COLS), D_MODEL ** -0.5),
        'w_pool_grp': nrm(ks[10], (DEPTH, N_POOL_GROUPS, POOL_GROUP, POOL_GROUP), POOL_GROUP ** -0.5),
        'pool_scale': 1.0 + nrm(ks[11], (DEPTH, POOL_WIDTH), 0.1),
        'conv_w': nrm(ks[12], (DEPTH, CONV_WIDTH, CONV_DIM), CONV_WIDTH ** -0.5),
        'conv_b': nrm(ks[13], (DEPTH, CONV_DIM), 0.02),
        'dt_bias': dt_init + jnp.log(-jnp.expm1(-dt_init)),
        'a_log': jnp.log(jax.random.uniform(ks[15], (DEPTH, SSD_HEADS), f32, 1.0, 16.0)),
        'd_skip': 1.0 + nrm(ks[16], (DEPTH, SSD_HEADS), 0.1),
        'ssd_norm_w': 1.0 + nrm(ks[17], (DEPTH, SSD_WIDTH), 0.02),
        'mem_norm_w': 1.0 + nrm(ks[18], (DEPTH, D_MODEL), 0.02),
        'w_mem_k': nrm(ks[19], (DEPTH, D_MODEL, ATT_WIDTH), D_MODEL ** -0.5),
        'w_mem_v': nrm(ks[20], (DEPTH, D_MODEL, ATT_WIDTH), D_MODEL ** -0.5),
        'w_pool_out': nrm(ks[21], (DEPTH, POOL_WIDTH, D_MODEL), POOL_WIDTH ** -0.5),
        'w_ssd_out': nrm(ks[22], (DEPTH, SSD_WIDTH, D_MODEL), SSD_WIDTH ** -0.5),
        'w_att_out': nrm(ks[23], (DEPTH, ATT_WIDTH, D_MODEL), ATT_WIDTH ** -0.5),
        'w_out': nrm(ks[24], (DEPTH, D_MODEL, D_MODEL), D_MODEL ** -0.5),
        'final_norm_w': 1.0 + nrm(ks[25], (D_MODEL,), 0.02),
    }


def reference(x_prompt, x_sample, mem_prompt, state_pool, state_conv, state_ssm, cache_mem_k, cache_mem_v,
              norm_w, w_in, w_pool_grp, pool_scale, conv_w, conv_b, dt_bias, a_log, d_skip, ssd_norm_w,
              mem_norm_w, w_mem_k, w_mem_v, w_pool_out, w_ssd_out, w_att_out, w_out, final_norm_w):
    xp, xs = x_prompt, x_sample
    bp = xp.shape[0]
    pool_p, conv_p, ssm_p, mk_p, mv_p = [], [], [], [], []
    pool_s, conv_s, ssm_s = [], [], []
    for l in range(DEPTH):
        lw = (norm_w[l], w_in[l], w_pool_grp[l], pool_scale[l], conv_w[l], conv_b[l], dt_bias[l], a_log[l],
              d_skip[l], ssd_norm_w[l], w_pool_out[l], w_ssd_out[l], w_att_out[l], w_out[l])
        mk, mv = _mem_kv(mem_prompt, mem_norm_w[l], w_mem_k[l], w_mem_v[l])
        hp0 = jnp.zeros((bp, POOL_HIST, POOL_WIDTH), xp.dtype)
        hc0 = jnp.zeros((bp, CONV_WIDTH - 1, CONV_DIM), xp.dtype)
        hs0 = jnp.zeros((bp, SSD_GROUPS, SSD_HPG, SSD_HEAD_DIM, SSD_STATE), xp.dtype)
        xp, sp, sc, ss = _layer(xp, hp0, hc0, hs0, mk, mv, 0, *lw)
        pool_p.append(sp); conv_p.append(sc); ssm_p.append(ss); mk_p.append(mk); mv_p.append(mv)
        xs, sp, sc, ss = _layer(xs, state_pool[l], state_conv[l], state_ssm[l], cache_mem_k[l], cache_mem_v[l],
                                PAST_LEN, *lw)
        pool_s.append(sp); conv_s.append(sc); ssm_s.append(ss)
    y_prompt = _rmsnorm(xp, final_norm_w)
    y_sample = _rmsnorm(xs, final_norm_w)
    return (y_prompt, y_sample, jnp.stack(pool_p), jnp.stack(conv_p), jnp.stack(ssm_p), jnp.stack(mk_p),
            jnp.stack(mv_p), jnp.stack(pool_s), jnp.stack(conv_s), jnp.stack(ssm_s))
```

```python
import numpy as np
import concourse.bass as bass
import concourse.mybir as mybir
from concourse.bass_utils import run_bass_kernel_spmd

F32 = mybir.dt.float32
BF16 = mybir.dt.bfloat16
AF = mybir.ActivationFunctionType
ALU = mybir.AluOpType
AX = mybir.AxisListType

ENGINES = ("pe", "act", "dve", "pool", "sp")
NDMA_SEM = 16
EPS = 1e-6

D = 1024
IN_COLS = 12320
OFF_U, OFF_ZP, OFF_ZS, OFF_XBC, OFF_DT, OFF_Q, OFF_ZA, OFF_G = 0, 1024, 2048, 4096, 7168, 7200, 8224, 9248


class Buf:
    __slots__ = ("name", "last_w", "readers", "dma_readers", "arena", "inherit", "claimed")

    def __init__(self, name, arena=False):
        self.name = name
        self.last_w = None
        self.readers = {}
        self.dma_readers = []
        self.arena = arena
        self.inherit = None
        self.claimed = False


class Op:
    __slots__ = ("eng", "fn", "deps", "is_dma", "sem", "semval", "signals", "sig", "prev_same_sem")

    def __init__(self, eng, fn, is_dma=False):
        self.eng = eng
        self.fn = fn
        self.deps = set()
        self.is_dma = is_dma
        self.sem = None
        self.semval = 0
        self.signals = False
        self.sig = 0
        self.prev_same_sem = None


class Prog:
    def __init__(self, nc):
        self.nc = nc
        self.streams = {e: [] for e in ENGINES}
        self.ndma = {e: 0 for e in ENGINES}
        self.dma_last = {}
        self.bufs = {}
        self.cur_barrier = None
        self.arena_dmas = []
        self.bar_t = None
        self.live = []

    def claim(self, b, lo, hi):
        inh = set()
        keep = []
        for (l2, h2, b2) in self.live:
            if l2 < hi and lo < h2 and b2 is not b:
                if b2.last_w is not None:
                    inh.add(b2.last_w)
                inh.update(b2.readers.values())
                inh.update(b2.dma_readers)
                if b2.inherit:
                    inh |= b2.inherit
                if not (lo <= l2 and h2 <= hi):
                    keep.append((l2, h2, b2))
            else:
                keep.append((l2, h2, b2))
        keep.append((lo, hi, b))
        self.live = keep
        b.inherit = (b.inherit or set()) | inh
        b.claimed = True

    def buf(self, name, arena=False):
        b = self.bufs.get(name)
        if b is None:
            b = self.bufs[name] = Buf(name, arena)
        return b

    def _track(self, o, reads, writes):
        rb = [self.buf(b) if not isinstance(b, Buf) else b for b in reads]
        wb = [self.buf(b) if not isinstance(b, Buf) else b for b in writes]
        ex = [b for b in rb if isinstance(b.name, tuple) and b.name[0] == "ps" and b not in wb]
        if ex:
            rb = [b for b in rb if b not in ex]
            wb = wb + ex
        arena = False
        for b in rb + wb:
            if b.arena:
                assert b.claimed, ("unclaimed arena buffer", b.name)
            if b.inherit:
                o.deps |= b.inherit
        for b in wb:
            b.inherit = None
        for b in rb:
            arena = arena or b.arena
            if b.last_w is not None:
                o.deps.add(b.last_w)
        for b in wb:
            arena = arena or b.arena
            if b.last_w is not None:
                o.deps.add(b.last_w)
            o.deps.update(b.readers.values())
            o.deps.update(b.dma_readers)
        for b in rb:
            if o.is_dma:
                b.dma_readers.append(o)
            else:
                b.readers[o.eng] = o
        for b in wb:
            b.last_w = o
            b.readers = {}
            b.dma_readers = []
        if arena and self.cur_barrier is not None:
            o.deps.add(self.cur_barrier)
        if arena and o.is_dma:
            self.arena_dmas.append(o)
        o.deps.discard(o)

    def op(self, eng, fn, reads=(), writes=()):
        o = Op(eng, fn)
        self._track(o, reads, writes)
        if eng == "pe":
            o.deps = {d for d in o.deps if d.is_dma or d.eng != "pe"}
        self.streams[eng].append(o)
        return o

    def dma(self, eng, out, in_, reads=(), writes=()):
        o = Op(eng, lambda e: e.dma_start(out=out, in_=in_), is_dma=True)
        self._track(o, reads, writes)
        slot = self.ndma[eng] % NDMA_SEM
        self.ndma[eng] += 1
        prev = self.dma_last.get((eng, slot))
        o.prev_same_sem = prev
        o.sem = (eng, slot)
        o.semval = (prev.semval if prev is not None else 0) + 16
        self.dma_last[(eng, slot)] = o
        self.streams[eng].append(o)
        return o

    def barrier(self):
        t = self.bar_t
        o = Op("dve", lambda e: e.memset(t[:, 0:1], 0.0))
        for e in ENGINES:
            for p in reversed(self.streams[e]):
                if not p.is_dma:
                    o.deps.add(p)
                    break
        o.deps.update(self.arena_dmas)
        self.arena_dmas = []
        self.streams["dve"].append(o)
        self.cur_barrier = o
        return o

    def emit(self):
        nc = self.nc
        for e in ENGINES:
            for o in self.streams[e]:
                for d in o.deps:
                    if not d.is_dma:
                        d.signals = True
        for e in ENGINES:
            c = 0
            for o in self.streams[e]:
                if not o.is_dma and o.signals:
                    c += 1
                    o.sig = c
        esem = {e: nc.alloc_semaphore("es_" + e) for e in ENGINES}
        dsem = {}
        for e in ENGINES:
            for s in range(min(NDMA_SEM, self.ndma[e])):
                dsem[(e, s)] = nc.alloc_semaphore("ds_%s_%d" % (e, s))
        streams = self.streams
        dma_last = self.dma_last
        stats = {}

        def run(ename, eng):
            waited = {}
            nw = 0

            def wait(key, sem, val):
                nonlocal nw
                if waited.get(key, 0) >= val:
                    return
                eng.wait_ge(sem, val)
                waited[key] = val
                nw += 1

            for o in streams[ename]:
                if o.is_dma and o.prev_same_sem is not None:
                    p = o.prev_same_sem
                    wait(("d",) + p.sem, dsem[p.sem], p.semval)
                for d in sorted(o.deps, key=lambda d: (d.is_dma, d.eng, d.sig, d.semval)):
                    if d.is_dma:
                        wait(("d",) + d.sem, dsem[d.sem], d.semval)
                    else:
                        wait(("e", d.eng), esem[d.eng], d.sig)
                ins = o.fn(eng)
                if o.is_dma:
                    ins.then_inc(dsem[o.sem], 16)
                elif o.signals:
                    ins.then_inc(esem[ename], 1)
            for (e, s), o in dma_last.items():
                if e == ename:
                    wait(("d", e, s), dsem[(e, s)], o.semval)
            stats[ename] = (len(streams[ename]), nw)

        with nc.Block() as block:
            @block.sync
            def _(e):
                run("sp", e)

            @block.scalar
            def _(e):
                run("act", e)

            @block.vector
            def _(e):
                run("dve", e)

            @block.gpsimd
            def _(e):
                run("pool", e)

            @block.tensor
            def _(e):
                run("pe", e)
        return stats


class TT:
    def __init__(self, h, F):
        self.h = h
        self.F = F

    def a(self, off=0, *dims, p0=0, np_=128):
        if not dims:
            dims = ([1, self.F - off],)
        return bass.AP(self.h, p0 * self.F + off, [[self.F, np_]] + [list(d) for d in dims])


class SB:
    def __init__(self, nc):
        self.nc = nc
        self.ptr = 16512
        self.top = 229344
        self.n = 0
        self.peak = 0

    def alloc(self, name, cols, dtype):
        sz = 2 if dtype == BF16 else 4
        off = (self.ptr + 31) // 32 * 32
        self.ptr = off + cols * sz
        self.peak = max(self.peak, self.ptr)
        assert self.ptr <= self.top, ("SBUF overflow", name, self.ptr)
        self.n += 1
        h = self.nc.alloc_sbuf_tensor_at("%s_%d" % (name, self.n), [128, cols], dtype, offset=off)
        t = TT(h, cols)
        t.lo, t.hi, t.esz = off, off + cols * sz, sz
        return t

    def mark(self):
        return self.ptr

    def reset(self, m):
        self.ptr = m


class Ring:
    def __init__(self, items):
        self.items = items
        self.i = 0

    def next(self):
        r = self.items[self.i % len(self.items)]
        self.i += 1
        return r


def build(cfg=None):
    cfg = cfg or {}
    tiles = cfg.get("tiles", [("p", 0), ("p", 1), ("p", 2), ("p", 3), ("s", 0)])
    do_mem = cfg.get("mem", True)
    nc = bass.Bass("TRN2", target_bir_lowering=False)
    P = Prog(nc)
    sb = SB(nc)

    def din(name, rows, cols):
        return TT(nc.dram_tensor(name, [rows, cols], F32, kind="ExternalInput"), cols)

    def dout(name, rows, cols):
        return TT(nc.dram_tensor(name, [rows, cols], F32, kind="ExternalOutput"), cols)

    xp_d, xs_d, mem_d = din("xp", 2048, 1024), din("xs", 128, 1024), din("mem", 256, 1024)
    stp_d, stc_d, sts_d = din("stp", 240, 1024), din("stc", 48, 3072), din("sts", 32768, 128)
    ck_d, cv_d = din("ck", 4096, 1024), din("cv", 4096, 1024)
    w_in_d, w_grp_d = din("w_in", 1024, IN_COLS), din("w_grp", 1024, 256)
    w_mk_d, w_mv_d = din("w_mk", 1024, 1024), din("w_mv", 1024, 1024)
    w_po_d, w_so_d, w_ao_d, w_o_d = din("w_po", 1024, 1024), din("w_so", 2048, 1024), din("w_ao", 1024, 1024), din("w_o", 1024, 1024)
    nwT_d, mnwT_d, pscT_d = din("nwT", 128, 8), din("mnwT", 128, 8), din("pscT", 128, 8)
    cwT_d, cbT_d, snwT_d = din("cwT", 128, 96), din("cbT", 128, 24), din("snwT", 128, 16)
    fnw_d, dtb_d, alog_d, dsk_d = din("fnw", 1, 1024), din("dtb", 1, 32), din("alog", 1, 32), din("dsk", 1, 32)

    y_p_d, y_s_d = dout("y_p", 2048, 1024), dout("y_s", 128, 1024)
    pool_p_d, conv_p_d, ssm_p_d = dout("pool_p", 15, 1024), dout("conv_p", 3, 3072), dout("ssm_p", 2048, 128)
    mk_p_d, mv_p_d = dout("mk_p", 256, 1024), dout("mv_p", 256, 1024)
    pool_s_d, conv_s_d, ssm_s_d = dout("pool_s", 240, 1024), dout("conv_s", 48, 3072), dout("ssm_s", 32768, 128)
    dbg_outs = {}

    pf, pb = [], []
    for i in range(8):
        h = nc.alloc_psum_tensor("psb%d" % i, [128, 512], F32)
        pf.append(TT(h, 512))
        pb.append(TT(h.bitcast(BF16), 1024))
    PS = lambda i: ("ps", i)
    mmr, trr, auxr = Ring([0, 1, 2]), Ring([3, 4]), Ring([5, 6, 7])

    ident_f, ident_b = sb.alloc("ident_f", 128, F32), sb.alloc("ident_b", 128, BF16)
    tri_p, tri_s, ones_f, S_s = sb.alloc("tri_p", 128, F32), sb.alloc("tri_s", 128, F32), sb.alloc("ones_f", 128, F32), sb.alloc("S_s", 128, F32)
    seqmask = sb.alloc("seqmask", 16, F32)
    maskb_p, maskb_s = sb.alloc("maskb_p", 128, BF16), sb.alloc("maskb_s", 128, BF16)
    nwT, mnwT, pscT = sb.alloc("nwT", 8, F32), sb.alloc("mnwT", 8, F32), sb.alloc("pscT", 8, F32)
    cwT, cbT, snwT = sb.alloc("cwT", 96, F32), sb.alloc("cbT", 24, F32), sb.alloc("snwT", 16, F32)
    dtb_bc, A_bc, D_bc = sb.alloc("dtb_bc", 32, F32), sb.alloc("A_bc", 32, F32), sb.alloc("D_bc", 32, F32)
    wgrp = sb.alloc("wgrp", 2048, BF16)
    bar_t = sb.alloc("bar_t", 8, F32)
    P.bar_t = bar_t.h
    junk = sb.alloc("junk", 2048, BF16)
    st = sb.alloc("st", 64, F32)
    eps_t = sb.alloc("eps_t", 8, F32)
    hT = sb.alloc("hT", 8 * 512, BF16)
    NSLAB = 4
    slabs = [sb.alloc("slab%d" % i, 4096, BF16) for i in range(NSLAB)]
    slabr = Ring(list(range(NSLAB)))
    merged = sb.alloc("merged", 8 * 512, F32)
    mT = sb.alloc("mT", 8 * 512, BF16)
    ARENA0_S = sb.mark()
    kT = sb.alloc("kT", 8 * 256, BF16)
    v_b = sb.alloc("v_b", 2 * 1024, BF16)
    hst = sb.alloc("hst", 2048, F32)
    hst_b = sb.alloc("hst_b", 2048, BF16)
    phist = sb.alloc("phist", 8 * 15, F32)
    chist = sb.alloc("chist", 24 * 3, F32)
    ARENA0_P = sb.mark()
    for nm_, t_ in (("kT", kT), ("v_b", v_b), ("hst", hst), ("hst_b", hst_b), ("phist", phist), ("chist", chist)):
        P.claim(P.buf(nm_), t_.lo, t_.hi)

    def ap(t, off, *dims, p0=0, np_=128):
        return t.a(off, *dims, p0=p0, np_=np_)

    def dbg(name, t, cols, reads):
        if not cfg.get("dbg"):
            return
        d = TT(nc.dram_tensor("dbg_" + name, [128, cols], F32, kind="ExternalOutput"), cols)
        stg = sb.alloc("dbgs_" + name, cols, F32)
        P.op("dve", lambda e: e.tensor_copy(out=stg.a(0), in_=t), reads=reads, writes=[P.buf("dbgs_" + name, True)])
        P.dma("sp", d.a(0), stg.a(0), reads=[P.buf("dbgs_" + name, True)])
        dbg_outs[name] = cols

    def mm(out, lhsT, rhs, start, stop, reads, writes, skip=False):
        if skip:
            P.op("pe", lambda e: e.matmul(out, lhsT=lhsT, rhs=rhs, start=start, stop=stop, skip_group_check=True), reads, writes)
        else:
            P.op("pe", lambda e: e.matmul(out, lhsT=lhsT, rhs=rhs, start=start, stop=stop), reads, writes)

    def tr(out, in_, ident, reads, writes):
        P.op("pe", lambda e: e.transpose(out, in_, ident), reads, writes)

    def act(out, in_, func, reads, writes, bias=None, scale=None, accum=None):
        kw = {}
        if bias is not None:
            kw["bias"] = bias
        if scale is not None:
            kw["scale"] = scale
        if accum is not None:
            kw["accum_out"] = accum
        P.op("act", lambda e: e.activation(out=out, in_=in_, func=func, **kw), reads, writes)

    def tt(eng, out, in0, in1, op, reads, writes):
        P.op(eng, lambda e: e.tensor_tensor(out=out, in0=in0, in1=in1, op=op), reads, writes)

    def ts(eng, out, in0, s1, s2, op0, op1, reads, writes):
        if s2 is None:
            P.op(eng, lambda e: e.tensor_scalar(out=out, in0=in0, scalar1=s1, scalar2=None, op0=op0), reads, writes)
        else:
            P.op(eng, lambda e: e.tensor_scalar(out=out, in0=in0, scalar1=s1, scalar2=s2, op0=op0, op1=op1), reads, writes)

    def stt(eng, out, in0, scalar, in1, op0, op1, reads, writes):
        P.op(eng, lambda e: e.scalar_tensor_tensor(out=out, in0=in0, scalar=scalar, in1=in1, op0=op0, op1=op1), reads, writes)

    def cp(eng, out, in_, reads, writes):
        if eng == "act":
            P.op("act", lambda e: e.activation(out=out, in_=in_, func=AF.Copy), reads, writes)
        else:
            P.op(eng, lambda e: e.tensor_copy(out=out, in_=in_), reads, writes)

    def memset(eng, out, val, writes, reads=()):
        P.op(eng, lambda e: e.memset(out, val), reads, writes)

    def asel(out, in_, pattern, cmp_, fill, base, cm, reads, writes):
        P.op("pool", lambda e: e.affine_select(out=out, in_=in_, pattern=pattern, compare_op=cmp_, fill=fill, base=base,
                                               channel_multiplier=cm), reads, writes)

    def recip(out, in_, reads, writes):
        P.op("dve", lambda e: e.reciprocal(out=out, in_=in_), reads, writes)

    def load_slab(dr, ncols_total, row0, col0, nk=8, ncols=512):
        i = slabr.next()
        if nk == 8 and ncols == 512 and cfg.get("split_slab", True):
            bufs = []
            for hf_ in range(2):
                o = slabs[i].a(hf_ * 4 * 512, [512, 4], [1, ncols])
                src = bass.AP(dr.h, (row0 + hf_ * 512) * ncols_total + col0, [[ncols_total, 128], [128 * ncols_total, 4], [1, ncols]])
                P.dma("pool", o, src, writes=[("slab", i, hf_)])
                bufs += [("slab", i, hf_)] * 4
            return slabs[i], bufs
        o = slabs[i].a(0, [512, nk], [1, ncols])
        src = bass.AP(dr.h, row0 * ncols_total + col0, [[ncols_total, 128], [128 * ncols_total, nk], [1, ncols]])
        P.dma("pool", o, src, writes=[("slab", i, 0), ("slab", i, 1)])
        return slabs[i], [("slab", i, 0)] * 4 + [("slab", i, 1)] * 4

    memset("pool", ident_f.a(0), 0.0, ["ident_f"])
    asel(ident_f.a(0), ident_f.a(0), [[-1, 128]], ALU.not_equal, 1.0, 0, 1, ["ident_f"], ["ident_f"])
    cp("pool", ident_b.a(0), ident_f.a(0), ["ident_f"], ["ident_b"])
    memset("pool", ones_f.a(0), 1.0, ["ones_f"])
    memset("pool", tri_p.a(0), 1.0, ["tri_p"])
    asel(tri_p.a(0), tri_p.a(0), [[1, 128]], ALU.is_ge, 0.0, 0, -1, ["tri_p"], ["tri_p"])
    memset("pool", S_s.a(0), 1.0, ["S_s"])
    asel(S_s.a(0, [8, 16], [1, 8]), S_s.a(0, [8, 16], [1, 8]), [[-8, 16], [0, 8]], ALU.is_ge, 0.0, 0, 1, ["S_s"], ["S_s"])
    asel(S_s.a(0, [8, 16], [1, 8]), S_s.a(0, [8, 16], [1, 8]), [[8, 16], [0, 8]], ALU.is_ge, 0.0, 7, -1, ["S_s"], ["S_s"])
    asel(tri_s.a(0), S_s.a(0), [[1, 128]], ALU.is_ge, 0.0, 0, -1, ["S_s"], ["tri_s"])
    ts("pool", maskb_p.a(0), tri_p.a(0), -1.0, 30000.0, ALU.add, ALU.mult, ["tri_p"], ["maskb_p"])
    ts("pool", maskb_s.a(0), tri_s.a(0), -1.0, 30000.0, ALU.add, ALU.mult, ["tri_s"], ["maskb_s"])
    memset("pool", seqmask.a(0), 1.0, ["seqmask"])
    asel(seqmask.a(0), seqmask.a(0), [[-8, 16]], ALU.is_ge, 0.0, 0, 1, ["seqmask"], ["seqmask"])
    asel(seqmask.a(0), seqmask.a(0), [[8, 16]], ALU.is_ge, 0.0, 7, -1, ["seqmask"], ["seqmask"])
    memset("pool", phist.a(0), 0.0, ["phist"])
    memset("pool", chist.a(0), 0.0, ["chist"])
    memset("pool", hst.a(0), 0.0, ["hst"])
    memset("pool", hst_b.a(0), 0.0, ["hst_b"])
    pnames = []
    for t_, d_ in ((nwT, nwT_d), (mnwT, mnwT_d), (pscT, pscT_d), (cwT, cwT_d), (cbT, cbT_d), (snwT, snwT_d)):
        pnames.append("params%d" % len(pnames))
        P.dma("sp", t_.a(0), d_.a(0), writes=[pnames[-1]])
    for t_, d_, n_ in ((dtb_bc, dtb_d, 32), (A_bc, alog_d, 32), (D_bc, dsk_d, 32)):
        pnames.append("params%d" % len(pnames))
        P.dma("sp", t_.a(0), bass.AP(d_.h, 0, [[0, 128], [1, n_]]), writes=[pnames[-1]])
    memset("dve", bar_t.a(4, [1, 1]), 0.0, ["params"], reads=pnames)
    act(A_bc.a(0), A_bc.a(0), AF.Exp, ["params"], ["params"])
    ts("dve", A_bc.a(0), A_bc.a(0), -1.0, None, ALU.mult, None, ["params"], ["params"])
    P.dma("pool", wgrp.a(0, [256, 8], [1, 256]), bass.AP(w_grp_d.h, 0, [[256, 128], [128 * 256, 8], [1, 256]]), writes=["wgrp"])

    strr = Ring([0, 1, 2, 3])

    def rstd_of(xin, xin_b, width):
        i = strr.next()
        bs_ = ("st", i)
        act(junk.a(0, [1, width]), xin, AF.Square, [xin_b], ["junk", bs_], accum=st.a(i * 4, [1, 1]))
        ts("dve", st.a(i * 4 + 1, [1, 1]), st.a(i * 4, [1, 1]), 1.0 / width, EPS, ALU.mult, ALU.add, [bs_], [bs_])
        act(st.a(i * 4 + 1, [1, 1]), st.a(i * 4 + 1, [1, 1]), AF.Ln, [bs_], [bs_])
        act(st.a(i * 4 + 1, [1, 1]), st.a(i * 4 + 1, [1, 1]), AF.Exp, [bs_], [bs_], scale=-0.5)
        return st.a(i * 4 + 1, [1, 1]), bs_

    def rms_rows(xin, xin_b, xout, xout_b, width):
        r, rb_ = rstd_of(xin, xin_b, width)
        ts("dve", xout, xin, r, None, ALU.mult, None, [xin_b, rb_], [xout_b])

    memset("pool", eps_t.a(0), EPS, ["eps_t"])
    EPS_AP = eps_t.a(0, [1, 1])

    def norm_transpose(x_rows_ap, dstT, dst_buf, col0, wT, T_, xt, xn, bx, bn):
        P.dma("sp", xt.a(0), x_rows_ap, writes=[bx])
        rms_rows(xt.a(0), bx, xn.a(0), bn, 1024)
        for half in range(2):
            b = trr.next()
            for c4 in range(4):
                c = half * 4 + c4
                tr(pf[b].a(c4 * 128, [1, 128]), xn.a(c * 128, [1, 128]), ident_f.a(0), [bn, "ident_f"], [PS(b)])
            tt("dve", dstT.a(half * 4 * T_ + col0, [T_, 4], [1, 128]), pf[b].a(0, [128, 4], [1, 128]),
               wT.a(half * 4, [1, 4], [0, 128]), ALU.mult, [PS(b), "params"], [dst_buf])

    def transpose_to(xn, bn, dstT, dst_buf, col0, wT, T_):
        for half in range(2):
            b = trr.next()
            for c4 in range(4):
                c = half * 4 + c4
                tr(pf[b].a(c4 * 128, [1, 128]), xn.a(c * 128, [1, 128]), ident_f.a(0), [bn, "ident_f"], [PS(b)])
            tt("dve", dstT.a(half * 4 * T_ + col0, [T_, 4], [1, 128]), pf[b].a(0, [128, 4], [1, 128]),
               wT.a(half * 4, [1, 4], [0, 128]), ALU.mult, [PS(b), "params"], [dst_buf])

    prenorm = {}
    PRE_BASE = None

    def phaseA_norm(kind2, ti2):
        prm2 = kind2 == "p"
        nch2 = 4 if prm2 else 1
        x_d2 = xp_d if prm2 else xs_d
        row02 = ti2 * 512 if prm2 else 0
        save = sb.ptr
        sb.ptr = ARENA0_P + 72 * 1024
        tg = (kind2, ti2)

        def alp(name, bn):
            t = sb.alloc(name, 1024, F32)
            bb_ = P.buf((tg, bn), True)
            P.claim(bb_, t.lo, t.hi)
            return t, bb_
        xts_ = [alp("xt", ("xt", i_)) for i_ in range(2)]
        outs = []
        for j in range(nch2):
            xn_, bn_ = alp("xn", ("xn", j))
            xt_, bx_ = xts_[j % 2]
            P.dma("sp", xt_.a(0), x_d2.a((row02 + j * 128) * 1024, [1, 1024]), writes=[bx_])
            rms_rows(xt_.a(0), bx_, xn_.a(0), bn_, 1024)
            outs.append((xn_, bn_))
        sb.ptr = save
        prenorm[tg] = outs

    if cfg.get("stop") == "const":
        do_mem = False
    if do_mem:
        m0 = ARENA0_P
        sb.reset(m0)
        Bm = lambda n: P.buf(("mem", n), True)

        def alm(name, cols, dtype, bn=None):
            t = sb.alloc(name, cols, dtype)
            P.claim(Bm(bn if bn is not None else name), t.lo, t.hi)
            return t
        mhT = alm("mhT", 8 * 256, BF16)
        kvf = alm("kvf", 2 * 1024, F32)
        k_bf = alm("k_bf", 2 * 1024, BF16)
        for mc in range(2):
            xt_, xn_ = alm("xt", 1024, F32, ("xt", mc)), alm("xn", 1024, F32, ("xn", mc))
            norm_transpose(mem_d.a(mc * 128 * 1024, [1, 1024]), mhT, Bm("mhT"), mc * 128, mnwT, 256, xt_, xn_,
                           Bm(("xt", mc)), Bm(("xn", mc)))
        for wi, (W, od) in enumerate(((w_mk_d, mk_p_d), (w_mv_d, mv_p_d)) if cfg.get("stop") != "memnorm" else ()):
            for cb in range(2):
                sl, slb = load_slab(W, 1024, 0, cb * 512)
                for mc in range(2):
                    b = mmr.next()
                    for k in range(8):
                        mm(pf[b].a(0), mhT.a(k * 256 + mc * 128, [1, 128]), sl.a(k * 512, [1, 512]), k == 0, k == 7,
                           [Bm("mhT"), slb[k]], [PS(b)])
                    cp("act", kvf.a(mc * 1024 + cb * 512, [1, 512]), pf[b].a(0), [PS(b)], [Bm("kvf")])
                    dst = k_bf if wi == 0 else v_b
                    dstb = Bm("k_bf") if wi == 0 else "v_b"
                    cp("dve", dst.a(mc * 1024 + cb * 512, [1, 512]), pf[b].a(0), [PS(b)], [dstb])
            for mc in range(2):
                P.dma("pool", od.a(mc * 128 * 1024, [1, 1024]), kvf.a(mc * 1024, [1, 1024]), reads=[Bm("kvf")])
        for mc in range(2 if cfg.get("stop") != "memnorm" else 0):
            b = trr.next()
            for i in range(8):
                tr(pb[b].a(i * 128, [1, 128]), k_bf.a(mc * 1024 + i * 128, [1, 128]), ident_b.a(0), [Bm("k_bf"), "ident_b"], [PS(b)])
            cp("act", kT.a(mc * 128, [256, 8], [1, 128]), pb[b].a(0, [128, 8], [1, 128]), [PS(b)], ["kT"])
        sb.reset(m0)

    def fm(sl, slb, e4, T_, extra_reads=()):
        b = mmr.next()
        for k in range(8):
            mm(pf[b].a(0, [1, T_]), sl.a(k * 512 + e4 * 128, [1, 128]), hT.a(k * 512, [1, T_]), k == 0, k == 7,
               [slb[k], "hT"], [PS(b)])
        return b

    def tmm(sl, slb, j, ncols=512):
        b = mmr.next()
        for k in range(8):
            mm(pf[b].a(0, [1, ncols]), hT.a(k * 512 + j * 128, [1, 128]), sl.a(k * 512, [1, ncols]), k == 0, k == 7,
               [slb[k], "hT"], [PS(b)])
        return b

    def tile(kind, ti, nxt_tile=None):
        prm = kind == "p"
        T_ = 512 if prm else 128
        NCH = T_ // 128
        NS, L = (1, 512) if prm else (16, 8)
        tag = (kind, ti)
        x_d = xp_d if prm else xs_d
        y_d = y_p_d if prm else y_s_d
        row0 = ti * 512 if prm else 0
        last_p = prm and ti == 3
        need_tm = last_p or not prm
        jl = NCH - 1
        TS = T_

        def AB(n):
            return P.buf((tag, n), True)

        def phase():
            sb.reset(ARENA0_P if prm else ARENA0_S)

        def al(name, cols, dtype, bn=None, parts=None):
            t = sb.alloc(name, cols, dtype)
            if parts is None:
                P.claim(AB(bn if bn is not None else name), t.lo, t.hi)
            else:
                for (bname, off, n) in parts:
                    P.claim(AB(bname), t.lo + off * t.esz, t.lo + (off + n) * t.esz)
            return t

        def ring_alloc(name, cols, dtype, n):
            return Ring([(al(name, cols, dtype, bn=(name, i)), AB((name, i))) for i in range(n)])

        phase()
        if prm:
            sb.ptr = ARENA0_P + 72 * 1024
        if tag in prenorm:
            for j, (xn_, bn_) in enumerate(prenorm[tag]):
                transpose_to(xn_, bn_, hT, "hT", j * 128, nwT, 512)
        else:
            xts = [(al("xt", 1024, F32, bn=("xt", i_)), al("xn", 1024, F32, bn=("xn", i_))) for i_ in range(2)]
            for j in range(NCH):
                xt_, xn_ = xts[j % 2]
                norm_transpose(x_d.a((row0 + j * 128) * 1024, [1, 1024]), hT, "hT", j * 128, nwT, 512, xt_, xn_,
                               AB(("xt", j % 2)), AB(("xn", j % 2)))

        def tm_rows(sl, slb, stage_ring, emit_rows):
            b = tmm(sl, slb, jl)
            stg, stgb = stage_ring.next()
            cp("act", stg.a(0), pf[b].a(0), [PS(b)], [stgb])
            emit_rows(stg, stgb)

        def gate_proj(bidx, yT, yTb, nk, w_d, first, last, gts, tmps):
            for half in range(2):
                gs, gsb = load_slab(w_in_d, IN_COLS, 0, OFF_G + bidx * 1024 + half * 512)
                wos = [load_slab(w_d, 1024, kk * 1024, half * 512) for kk in range(nk // 8)]
                for c4 in range(4):
                    c = half * 4 + c4
                    bg = fm(gs, gsb, c4, T_)
                    gt, gtb = gts.next()
                    act(gt.a(0, [1, T_]), pf[bg].a(0, [1, T_]), AF.Sigmoid, [PS(bg)], [gtb])
                    bp = mmr.next()
                    for k in range(nk):
                        wsl, wslb = wos[k // 8]
                        mm(pf[bp].a(0, [1, T_]), wsl.a((k % 8) * 512 + c4 * 128, [1, 128]), yT.a(k * TS, [1, T_]),
                           k == 0, k == nk - 1, [wslb[k % 8], yTb], [PS(bp)])
                    mg = merged.a(c * 512, [1, T_])
                    if first:
                        tt("dve", mg, pf[bp].a(0, [1, T_]), gt.a(0, [1, T_]), ALU.mult, [PS(bp), gtb], [("merged", c)])
                    else:
                        tp, tpb = tmps.next()
                        tt("dve", tp.a(0, [1, T_]), pf[bp].a(0, [1, T_]), gt.a(0, [1, T_]), ALU.mult, [PS(bp), gtb], [tpb])
                        if last:
                            tt("dve", mT.a(c * 512, [1, T_]), mg, tp.a(0, [1, T_]), ALU.add, [("merged", c), tpb], ["mT"])
                        else:
                            tt("dve", mg, mg, tp.a(0, [1, T_]), ALU.add, [("merged", c), tpb], [("merged", c)])


        if cfg.get("nphase", 9) < 2:
            return
        phase()
        Lx = 15 + L
        EXW = NS * Lx
        pext = [al("pext", EXW, F32, bn=("pext", e_)) for e_ in range(8)]
        ta, tb = al("ta", EXW, F32), al("tb", EXW, F32)
        dT = al("dT", 8 * TS, BF16)
        szp = al("szp", 8 * TS, F32)
        ypool = al("ypool", 8 * TS, BF16)
        gts = ring_alloc("gt", TS, F32, 2)
        tmps = ring_alloc("tmpm", TS, F32, 2)
        stages = ring_alloc("tmstage", 512, F32, 1) if need_tm else None
        if prm:
            for e in range(8):
                cp("dve", pext[e].a(0, [1, 15]), phist.a(e * 15, [1, 15]), ["phist"], [AB(("pext", e))])
        else:
            r0, r1 = al("sp_r0", 1024, F32), al("sp_r1", 1024, F32)
            P.dma("sp", r0.a(0), stp_d.a(0, [1, 1024]), writes=[AB("sp_r0")])
            P.dma("sp", r1.a(0, [1, 1024], np_=112), stp_d.a(128 * 1024, [1, 1024], np_=112), writes=[AB("sp_r1")])
            for e in range(8):
                b = trr.next()
                tr(pf[b].a(0, [1, 128]), r0.a(e * 128, [1, 128]), ident_f.a(0), [AB("sp_r0"), "ident_f"], [PS(b)])
                tr(pf[b].a(128, [1, 112]), r1.a(e * 128, [1, 128], np_=112), ident_f.a(0, [1, 112], np_=112),
                   [AB("sp_r1"), "ident_f"], [PS(b)])
                cp("dve", pext[e].a(0, [Lx, 16], [1, 15]), pf[b].a(0, [15, 16], [1, 15]), [PS(b)], [AB(("pext", e))])
            P.dma("sp", bass.AP(pool_s_d.h, 0, [[15 * 1024, 16], [1024, 7], [1, 1024]]),
                  bass.AP(stp_d.h, 8 * 1024, [[15 * 1024, 16], [1024, 7], [1, 1024]]))
        for s2 in range(2):
            sl, slb = load_slab(w_in_d, IN_COLS, 0, OFF_U + s2 * 512)
            for e4 in range(4):
                e = s2 * 4 + e4
                b = fm(sl, slb, e4, T_)
                cp("act", pext[e].a(15, [Lx, NS], [1, L]), pf[b].a(0, [L, NS], [1, L]), [PS(b)], [AB(("pext", e))])
                if prm:
                    cp("dve", phist.a(e * 15, [1, 15]), pext[e].a(L, [1, 15]), [AB(("pext", e))], ["phist"])
            if need_tm:
                def rows_u(stg, stgb, s2=s2):
                    if prm:
                        P.dma("sp", pool_p_d.a(s2 * 512, [1, 512], np_=15), stg.a(0, [1, 512], p0=113, np_=15), reads=[stgb])
                    else:
                        for t8 in range(8):
                            P.dma("sp", bass.AP(pool_s_d.h, (7 + t8) * 1024 + s2 * 512, [[15 * 1024, 16], [1, 512]]),
                                  bass.AP(stg.h, t8 * 512, [[8 * 512, 16], [1, 512]]), reads=[stgb])
                tm_rows(sl, slb, stages, rows_u)
        for e in range(8):
            w = (2, 2, 4, 4, 8, 8, 16, 16)[e]
            x_ = pext[e]
            xb_ = AB(("pext", e))

            def R(t, a0, n):
                return t.a(a0, [Lx, NS], [1, n])
            tt("dve", R(ta, 1, Lx - 1), R(x_, 1, Lx - 1), R(x_, 0, Lx - 1), ALU.add, [xb_], [AB("ta")])
            cur, curb = ta, AB("ta")
            if w >= 4:
                tt("dve", R(tb, 3, Lx - 3), R(ta, 3, Lx - 3), R(ta, 1, Lx - 3), ALU.add, [AB("ta")], [AB("tb")])
                cur, curb = tb, AB("tb")
            if w >= 8:
                tt("dve", R(ta, 7, Lx - 7), R(tb, 7, Lx - 7), R(tb, 3, Lx - 7), ALU.add, [AB("tb")], [AB("ta")])
                cur, curb = ta, AB("ta")
            if w >= 16:
                tt("dve", R(tb, 15, Lx - 15), R(ta, 15, Lx - 15), R(ta, 7, Lx - 15), ALU.add, [AB("ta")], [AB("tb")])
                cur, curb = tb, AB("tb")
            stt("dve", dT.a(e * TS, [L, NS], [1, L]), R(cur, 15, L), 1.0 / w, R(x_, 15, L), ALU.mult, ALU.subtract,
                [curb, xb_], [AB("dT")])
            if prm and ti == 0:
                for t in range(w - 1):
                    stt("dve", dT.a(e * TS + t, [1, 1]), cur.a(15 + t, [1, 1]), 1.0 / (t + 1), x_.a(15 + t, [1, 1]),
                        ALU.mult, ALU.subtract, [curb, xb_, AB("dT")], [AB("dT")])
        for s2 in range(2):
            sl, slb = load_slab(w_in_d, IN_COLS, 0, OFF_ZP + s2 * 512)
            for e4 in range(4):
                e = s2 * 4 + e4
                b = fm(sl, slb, e4, T_)
                act(szp.a(e * TS, [1, T_]), pf[b].a(0, [1, T_]), AF.Silu, [PS(b)], [AB("szp")])
        for g in range(4):
            for half in range(2):
                e = g * 2 + half
                b = mmr.next()
                for kc in range(2):
                    mm(pf[b].a(0, [1, T_]), wgrp.a((g * 2 + kc) * 256 + half * 128, [1, 128]), dT.a((g * 2 + kc) * TS, [1, T_]),
                       kc == 0, kc == 1, ["wgrp", AB("dT")], [PS(b)])
                stt("dve", ypool.a(e * TS, [1, T_]), pf[b].a(0, [1, T_]), pscT.a(e, [1, 1]), szp.a(e * TS, [1, T_]),
                    ALU.mult, ALU.mult, [PS(b), "params", AB("szp")], [AB("ypool")])
        gate_proj(0, ypool, AB("ypool"), 8, w_po_d, True, False, gts, tmps)

        if cfg.get("nphase", 9) < 3:
            return
        phase()
        qT = al("qT", 8 * TS, BF16)
        sza = al("sza", NCH * 1024, F32, parts=[(("sza", j_), j_ * 1024, 1024) for j_ in range(NCH)])
        yaT = al("yaT", 8 * TS, BF16)
        gts = ring_alloc("gt", TS, F32, 2)
        tmps = ring_alloc("tmpm", TS, F32, 2)
        p_ts = ring_alloc("p_t", 1024, BF16, 2)
        pTs = ring_alloc("pT", 1024, BF16, 2)
        yas = ring_alloc("ya", 1024, BF16, 2)
        ast_ = ring_alloc("ast", 16, F32, 2)
        if not prm:
            expb = al("expb", 8 * 16 * 128, BF16)
            Kbs = ring_alloc("Kb", 2048, BF16, 2)
            kTss = ring_alloc("kTs", 2048, BF16, 2)
            Vbs = ring_alloc("Vb", 2048, BF16, 2)
            memset("dve", expb.a(0), 0.0, [AB("expb")])
        for s2 in range(2):
            sl, slb = load_slab(w_in_d, IN_COLS, 0, OFF_Q + s2 * 512)
            for e4 in range(4):
                e = s2 * 4 + e4
                b = fm(sl, slb, e4, T_)
                cp("act", qT.a(e * TS, [1, T_]), pf[b].a(0, [1, T_]), [PS(b)], [AB("qT")])
        zasl = [load_slab(w_in_d, IN_COLS, 0, OFF_ZA + cb * 512) for cb in range(2)]
        zar = Ring([1, 2])

        def zaproj(j):
            for cb in range(2):
                b = zar.next()
                for k in range(8):
                    mm(pf[b].a(0), hT.a(k * 512 + j * 128, [1, 128]), zasl[cb][0].a(k * 512, [1, 512]), k == 0, k == 7,
                       [zasl[cb][1][k], "hT"], [PS(b)])
                act(sza.a(j * 1024 + cb * 512, [1, 512]), pf[b].a(0), AF.Silu, [PS(b)], [AB(("sza", j))])

        def attn(j):
            if prm:
                bs = [5, 6] if j % 2 == 0 else [7, 0]
                for h in range(4):
                    for dc in range(2):
                        i = h * 2 + dc
                        mm(pf[bs[h // 2]].a((h % 2) * 256, [1, 256]), qT.a(i * TS + j * 128, [1, 128]), kT.a(i * 256, [1, 256]),
                           dc == 0, dc == 1, [AB("qT"), "kT"], [PS(bs[h // 2])])
            else:
                bs = [6, 7]
                cp("dve", expb.a(0, [2048, 8], [136, 16], [1, 8]), qT.a(0, [TS, 8], [8, 16], [1, 8]), [AB("qT")], [AB("expb")])
                for bq in range(16):
                    Kb, Kbb = Kbs.next()
                    P.dma("pool", Kb.a(0, [1024, 2], [1, 1024]),
                          bass.AP(ck_d.h, bq * 256 * 1024, [[1024, 128], [128 * 1024, 2], [1, 1024]]), writes=[Kbb])
                    kTs, kTsb = kTss.next()
                    for mc in range(2):
                        bt = trr.next()
                        for i in range(8):
                            tr(pb[bt].a(i * 128, [1, 128]), Kb.a(mc * 1024 + i * 128, [1, 128]), ident_b.a(0), [Kbb, "ident_b"], [PS(bt)])
                        cp("act" if mc == 0 else "dve", kTs.a(mc * 128, [256, 8], [1, 128]), pb[bt].a(0, [128, 8], [1, 128]), [PS(bt)], [kTsb])
                    for h in range(4):
                        for dc in range(2):
                            i = h * 2 + dc
                            mm(pf[bs[h // 2]].a((h % 2) * 256, [1, 256]), expb.a(i * 2048 + bq * 128, [1, 128]), kTs.a(i * 256, [1, 256]),
                               bq == 0 and dc == 0 and h % 2 == 0, bq == 15 and dc == 1, [AB("expb"), kTsb], [PS(bs[h // 2])], skip=True)
            a_, ab_ = ast_.next()
            for hb in range(2):
                P.op("dve", (lambda o_, i_: (lambda e: e.tensor_reduce(out=o_, in_=i_, axis=AX.X, op=ALU.max)))(
                    a_.a(hb * 2, [1, 2]), pf[bs[hb]].a(0, [256, 2], [1, 256])), [PS(bs[hb])], [ab_])
            ts("dve", a_.a(4, [1, 4]), a_.a(0, [1, 4]), -1.0 / 16, None, ALU.mult, None, [ab_], [ab_])
            p_t, p_tb = p_ts.next()
            for h in range(4):
                act(p_t.a(h * 256, [1, 256]), pf[bs[h // 2]].a((h % 2) * 256, [1, 256]), AF.Exp, [PS(bs[h // 2]), ab_], [p_tb, ab_],
                    bias=a_.a(4 + h, [1, 1]), scale=1.0 / 16, accum=a_.a(8 + h, [1, 1]))
            recip(a_.a(12, [1, 4]), a_.a(8, [1, 4]), [ab_], [ab_])
            yield "A"
            bt = trr.next()
            for h in range(4):
                for mc in range(2):
                    tr(pb[bt].a((h * 2 + mc) * 128, [1, 128]), p_t.a(h * 256 + mc * 128, [1, 128]), ident_b.a(0), [p_tb, "ident_b"], [PS(bt)])
            pT, pTb = pTs.next()
            cp("act", pT.a(0), pb[bt].a(0), [PS(bt)], [pTb])
            if prm:
                bo = bs
                for h in range(4):
                    for mc in range(2):
                        mm(pf[bo[h // 2]].a((h % 2) * 256, [1, 256]), pT.a((h * 2 + mc) * 128, [1, 128]), v_b.a(mc * 1024 + h * 256, [1, 256]),
                           mc == 0, mc == 1, [pTb, "v_b"], [PS(bo[h // 2])])
            else:
                bo = [6, 7]
                cp("dve", expb.a(0, [2048, 8], [136, 16], [1, 8]), pT.a(0, [128, 8], [8, 16], [1, 8]), [pTb], [AB("expb")])
                for bq in range(16):
                    Vb, Vbb = Vbs.next()
                    P.dma("pool", Vb.a(0, [1024, 2], [1, 1024]),
                          bass.AP(cv_d.h, bq * 256 * 1024, [[1024, 128], [128 * 1024, 2], [1, 1024]]), writes=[Vbb])
                    for h in range(4):
                        for mc in range(2):
                            mm(pf[bo[h // 2]].a((h % 2) * 256, [1, 256]), expb.a((h * 2 + mc) * 2048 + bq * 128, [1, 128]),
                               Vb.a(mc * 1024 + h * 256, [1, 256]), bq == 0 and mc == 0 and h % 2 == 0, bq == 15 and mc == 1,
                               [AB("expb"), Vbb], [PS(bo[h // 2])], skip=True)
            ya, yab = yas.next()
            for h in range(4):
                stt("dve", ya.a(h * 256, [1, 256]), pf[bo[h // 2]].a((h % 2) * 256, [1, 256]), a_.a(12 + h, [1, 1]),
                    sza.a(j * 1024 + h * 256, [1, 256]), ALU.mult, ALU.mult, [PS(bo[h // 2]), ab_, AB(("sza", j))], [yab])
            yield "B"
            bt = trr.next()
            for c in range(8):
                tr(pb[bt].a(c * 128, [1, 128]), ya.a(c * 128, [1, 128]), ident_b.a(0), [yab, "ident_b"], [PS(bt)])
            cp("act", yaT.a(j * 128, [TS, 8], [1, 128]), pb[bt].a(0, [128, 8], [1, 128]), [PS(bt)], [AB("yaT")])

        def run_to(g_, marks):
            while True:
                try:
                    r = next(g_)
                except StopIteration:
                    return None
                if r in marks:
                    return r

        agens = [attn(j) for j in range(NCH)]
        if not prm:
            zaproj(0)
        run_to(agens[0], ("A",))
        for j in range(NCH):
            if prm:
                zaproj(j)
            if j + 1 < NCH:
                run_to(agens[j + 1], ("A",))
            run_to(agens[j], ("B",))
            run_to(agens[j], ())
        gate_proj(2, yaT, AB("yaT"), 8, w_ao_d, False, False, gts, tmps)

        if cfg.get("nphase", 9) < 4:
            return
        phase()
        W3 = 3 + L
        dtr = al("dtr", NCH * 32, F32)
        dtv = al("dtv", NCH * 32, F32)
        sp1 = al("sp1", NCH * 32, F32)
        av = al("av", NCH * 32, F32)
        xs_tm = al("xs_tm", NCH * 2048, BF16)
        B_tm = al("B_tm", NCH * 512, BF16)
        BT = al("BT", 4 * TS, BF16)
        CT = al("CT", 4 * TS, BF16)
        ysT = al("ysT", 16 * TS, BF16)
        gts = ring_alloc("gt", TS, F32, 2)
        tmps = ring_alloc("tmpm", TS, F32, 2)
        SUB0 = sb.mark()
        cexts = ring_alloc("cext", NS * W3, F32, 4)
        caccs = ring_alloc("cacc", TS, F32, 3)
        xcs = ring_alloc("xc", TS, BF16, 4)
        stages = ring_alloc("tmstage", 512, F32, 1) if need_tm else None
        if not prm:
            chs = al("chs", 24 * 48, F32)
            stc_sb = al("stc_sb", 3072, F32)
            P.dma("sp", stc_sb.a(0, [1, 3072], np_=48), stc_d.a(0, [1, 3072], np_=48), writes=[AB("stc_sb")])
            for e8 in range(3):
                b = trr.next()
                for i in range(8):
                    e = e8 * 8 + i
                    tr(pf[b].a(i * 48, [1, 48]), stc_sb.a(e * 128, [1, 128], np_=48), ident_f.a(0, [1, 48], np_=48),
                       [AB("stc_sb"), "ident_f"], [PS(b)])
                cp("dve", chs.a(e8 * 8 * 48, [1, 384]), pf[b].a(0, [1, 384]), [PS(b)], [AB("chs")])
        sl, slb = load_slab(w_in_d, IN_COLS, 0, OFF_DT, 8, 32)
        for j in range(NCH):
            b = tmm(sl, slb, j, 32)
            tt("dve", dtr.a(j * 32, [1, 32]), pf[b].a(0, [1, 32]), dtb_bc.a(0), ALU.add, [PS(b), "params"], [AB("dtr")])
        stt("dve", sp1.a(0), dtr.a(0), -1.0, dtr.a(0), ALU.mult, ALU.max, [AB("dtr")], [AB("sp1")])
        act(sp1.a(0), sp1.a(0), AF.Exp, [AB("sp1")], [AB("sp1")], scale=-1.0)
        act(sp1.a(0), sp1.a(0), AF.Ln, [AB("sp1")], [AB("sp1")], bias=1.0)
        stt("dve", dtv.a(0), dtr.a(0), 0.0, sp1.a(0), ALU.max, ALU.add, [AB("dtr"), AB("sp1")], [AB("dtv")])
        tt("dve", av.a(0, [32, NCH], [1, 32]), dtv.a(0, [32, NCH], [1, 32]), A_bc.a(0, [0, NCH], [1, 32]), ALU.mult,
           [AB("dtv"), "params"], [AB("av")])
        pendB = []

        def convB(e, xc, xcb):
            if e < 16:
                bt = trr.next()
                for jj in range(NCH):
                    tr(pb[bt].a(jj * 128, [1, 128]), xc.a(jj * 128, [1, 128]), ident_b.a(0), [xcb, "ident_b"], [PS(bt)])
                cp("act", xs_tm.a(e * 128, [2048, NCH], [1, 128]), pb[bt].a(0, [128, NCH], [1, 128]), [PS(bt)], [AB("xs_tm")])
            elif e < 20:
                g = e - 16
                bt = trr.next()
                for jj in range(NCH):
                    tr(pb[bt].a(jj * 128, [1, 128]), BT.a(g * TS + jj * 128, [1, 128]), ident_b.a(0), [AB("BT"), "ident_b"], [PS(bt)])
                cp("act", B_tm.a(g * 128, [512, NCH], [1, 128]), pb[bt].a(0, [128, NCH], [1, 128]), [PS(bt)], [AB("B_tm")])

        for s6 in range(6):
            sl, slb = load_slab(w_in_d, IN_COLS, 0, OFF_XBC + s6 * 512)
            for e4 in range(4):
                e = s6 * 4 + e4
                b = fm(sl, slb, e4, T_)
                ce, ceb = cexts.next()
                if prm:
                    cp("dve", ce.a(0, [1, 3]), chist.a(e * 3, [1, 3]), ["chist"], [ceb])
                else:
                    cp("dve", ce.a(0, [W3, 16], [1, 3]), chs.a(e * 48, [3, 16], [1, 3]), [AB("chs")], [ceb])
                cp("act", ce.a(3, [W3, NS], [1, L]), pf[b].a(0, [L, NS], [1, L]), [PS(b), ceb], [ceb])
                if prm:
                    cp("dve", chist.a(e * 3, [1, 3]), ce.a(L, [1, 3]), [ceb], ["chist"])
                ca, cab = caccs.next()
                cav = ca.a(0, [L, NS], [1, L])
                act(cav, ce.a(0, [W3, NS], [1, L]), AF.Identity, [ceb, "params"], [cab], bias=cbT.a(e, [1, 1]), scale=cwT.a(e * 4, [1, 1]))
                for k in range(1, 4):
                    stt("dve", cav, ce.a(k, [W3, NS], [1, L]), cwT.a(e * 4 + k, [1, 1]), cav, ALU.mult, ALU.add, [ceb, cab, "params"], [cab])
                if e < 16:
                    xc, xcb = xcs.next()
                    act(xc.a(0, [1, T_]), ca.a(0, [1, T_]), AF.Silu, [cab], [xcb])
                    pendB.append((e, xc, xcb))
                elif e < 20:
                    g = e - 16
                    act(BT.a(g * TS, [1, T_]), ca.a(0, [1, T_]), AF.Silu, [cab], [AB("BT")])
                    pendB.append((e, None, None))
                else:
                    g = e - 20
                    act(CT.a(g * TS, [1, T_]), ca.a(0, [1, T_]), AF.Silu, [cab], [AB("CT")])
                if len(pendB) > 3:
                    convB(*pendB.pop(0))
            if need_tm:
                def rows_c(stg, stgb, s6=s6):
                    if prm:
                        P.dma("sp", conv_p_d.a(s6 * 512, [1, 512], np_=3), stg.a(0, [1, 512], p0=125, np_=3), reads=[stgb])
                    else:
                        for j3 in range(3):
                            P.dma("sp", bass.AP(conv_s_d.h, j3 * 3072 + s6 * 512, [[3 * 3072, 16], [1, 512]]),
                                  bass.AP(stg.h, (5 + j3) * 512, [[8 * 512, 16], [1, 512]]), reads=[stgb])
                tm_rows(sl, slb, stages, rows_c)
        while pendB:
            convB(*pendB.pop(0))
        sb.reset(SUB0)
        smalls = ring_alloc("ssd_small", 32 * 8, F32, 2)
        acsTs = ring_alloc("acsT", 128, F32, 2)
        abfs = ring_alloc("abf", 256, BF16, 2)
        xdts = ring_alloc("xdt", 2048, BF16, 2 if prm else 1)
        xdds = ring_alloc("xdd", 2048, BF16, 2 if prm else 1)
        CBms = ring_alloc("CBm", 512, F32, 1)
        szss = ring_alloc("szs", 2048, F32, 1)
        Es = ring_alloc("E", 512, F32, 2)
        MTs = ring_alloc("MT", 512, BF16, 3)
        t1s = ring_alloc("t1", 512, F32, 2)
        t2s = ring_alloc("t2", 512, F32, 1)
        ys = ring_alloc("yv", 2048, F32, 1)
        yns = ring_alloc("yn", 2048, BF16, 1)
        zsl = [load_slab(w_in_d, IN_COLS, 0, OFF_ZS + cb * 512) for cb in range(4)]
        tri = tri_p if prm else tri_s
        trib = "tri_p" if prm else "tri_s"
        maskb, maskbn = (maskb_p, "maskb_p") if prm else (maskb_s, "maskb_s")
        Smat, Smb = (ones_f, "ones_f") if prm else (S_s, "S_s")

        def chunk(j):
            sm_, smb = smalls.next()
            acs, nacs, eacs, dlt, dec, etot, dtd, cdv = [sm_.a(i * 32, [1, 32]) for i in range(8)]
            a_j = av.a(j * 32, [1, 32])
            dt_j = dtv.a(j * 32, [1, 32])
            b1 = trr.next()
            mm(pf[b1].a(0, [1, 32]), tri.a(0), a_j, True, True, [trib, AB("av")], [PS(b1)])
            mm(pf[b1].a(32, [1, 32]), Smat.a(0), a_j, True, True, [Smb, AB("av")], [PS(b1)])
            mm(pf[b1].a(64, [1, 128], np_=32), a_j, tri.a(0), True, True, [trib, AB("av")], [PS(b1)])
            cp("dve", acs, pf[b1].a(0, [1, 32]), [PS(b1)], [smb])
            ts("dve", nacs, pf[b1].a(0, [1, 32]), -1.0, None, ALU.mult, None, [PS(b1)], [smb])
            act(eacs, pf[b1].a(0, [1, 32]), AF.Exp, [PS(b1)], [smb])
            tt("dve", dlt, pf[b1].a(32, [1, 32]), acs, ALU.subtract, [PS(b1), smb], [smb])
            act(dec, dlt, AF.Exp, [smb], [smb])
            act(etot, pf[b1].a(32, [1, 32]), AF.Exp, [PS(b1)], [smb])
            acsT, acsTb = acsTs.next()
            cp("act", acsT.a(0, [1, 128], np_=32), pf[b1].a(64, [1, 128], np_=32), [PS(b1)], [acsTb])
            abf, abfb = abfs.next()
            cp("dve", abf.a(0, [1, 128], np_=32), acsT.a(0, [1, 128], np_=32), [acsTb], [abfb])
            tt("dve", abf.a(128, [1, 128], np_=32), acsT.a(0, [1, 128], np_=32), abf.a(0, [1, 128], np_=32), ALU.subtract, [acsTb, abfb], [abfb])
            tt("dve", dtd, dt_j, dec, ALU.mult, [AB("dtv"), smb], [smb])
            xdt, xdtb = xdts.next()
            xdd, xddb = xdds.next()
            tt(cfg.get("pool_eng", "dve"), xdt.a(0, [64, 32], [1, 64]), xs_tm.a(j * 2048, [64, 32], [1, 64]), dtv.a(j * 32, [1, 32], [0, 64]), ALU.mult,
               [AB("xs_tm"), AB("dtv")], [xdtb])
            tt(cfg.get("pool_eng", "dve"), xdd.a(0, [64, 32], [1, 64]), xs_tm.a(j * 2048, [64, 32], [1, 64]), sm_.a(6 * 32, [1, 32], [0, 64]), ALU.mult,
               [AB("xs_tm"), smb], [xddb])
            bc = auxr.next()
            for g in range(4):
                mm(pf[bc].a(g * 128, [1, 128]), BT.a(g * TS + j * 128, [1, 128]), CT.a(g * TS + j * 128, [1, 128]), True, True,
                   [AB("BT"), AB("CT")], [PS(bc)])
            CBm, CBmb = CBms.next()
            tt("dve", CBm.a(0, [128, 4], [1, 128]), pf[bc].a(0, [128, 4], [1, 128]), tri.a(0, [0, 4], [1, 128]), ALU.mult,
               [PS(bc), trib], [CBmb])
            szs, szsb = szss.next()
            for cb in range(4):
                b = tmm(zsl[cb][0], zsl[cb][1], j)
                act(szs.a(cb * 512, [1, 512]), pf[b].a(0), AF.Silu, [PS(b)], [szsb])
            yield "HB"
            if not prm:
                etT = al("etT", 128, F32)
                cdN = al("cdN", 256, F32)
                CTbs = ring_alloc("CTb", 512, BF16, 2)
                h0ns = ring_alloc("h0n", 2048, F32, 3)
                h0bs = ring_alloc("h0b", 2048, BF16, 2)
                h0Ts = ring_alloc("h0T", 2048, BF16, 2)
                xddbs = ring_alloc("xdd_b", 2048, BF16, 2)
                mR = sb.mark()
                Rrep = al("Rrep", 2048, F32)
                memset("pool", Rrep.a(0, np_=32), 0.0, [AB("Rrep")])
                asel(Rrep.a(0, [128, 16], [64, 2], [1, 64], np_=32), Rrep.a(0, [128, 16], [64, 2], [1, 64], np_=32),
                     [[-2, 16], [-1, 2], [0, 64]], ALU.not_equal, 1.0, 0, 1, [AB("Rrep")], [AB("Rrep")])
                for CTb, CTbb in CTbs.items:
                    memset("pool", CTb.a(0), 0.0, [CTbb])
                bq_ = trr.next()
                tr(pf[bq_].a(0, [1, 128], np_=32), sm_.a(5 * 32, [1, 32]), ident_f.a(0), [smb, "ident_f"], [PS(bq_)])
                cp("dve", etT.a(0, [1, 128], np_=32), pf[bq_].a(0, [1, 128], np_=32), [PS(bq_)], [AB("etT")])
                bq_ = trr.next()
                for rc in range(16):
                    mm(pf[bq_].a(rc * 16, [1, 16]), Rrep.a(rc * 128, [1, 128], np_=32), etT.a(7, [8, 16], np_=32), True, True,
                       [AB("Rrep"), AB("etT")], [PS(bq_)])
                cp("dve", cdN.a(0), pf[bq_].a(0, [1, 256]), [PS(bq_)], [AB("cdN")])
                sb.reset(mR)
                hfs = ring_alloc("hf", 2048, F32, 2)
                yoff_banks = [0, 1, 2, 5]
                st_ring = Ring([6, 7])

                def make_xq(b_):
                    xq_, xqb_ = xddbs.next()
                    ts("dve", xq_.a(0), xdd.a(0), seqmask.a(b_, [1, 1]), None, ALU.mult, None, [xddb, "seqmask"], [xqb_])
                    return xq_, xqb_
                xq_next = make_xq(0)
                def seq_s1(bq):
                    h0n, h0nb = h0ns.next()
                    P.dma("sp", h0n.a(0, [128, 16], [1, 128]),
                          bass.AP(sts_d.h, bq * 2048 * 128, [[128, 128], [128 * 128, 16], [1, 128]]), writes=[h0nb])
                    h0b, h0bb = h0bs.next()
                    cp("act", h0b.a(0), h0n.a(0), [h0nb], [h0bb])
                    h0T, h0Tb = h0Ts.next()
                    for half in range(2):
                        bt = trr.next()
                        for i in range(8):
                            tr(pb[bt].a(i * 128, [1, 128]), h0b.a((half * 8 + i) * 128, [1, 128]), ident_b.a(0), [h0bb, "ident_b"], [PS(bt)])
                        cp("act", h0T.a(half * 1024, [1, 1024]), pb[bt].a(0), [PS(bt)], [h0Tb])
                    CTb, CTbb = CTbs.next()
                    cp("dve", CTb.a(bq * 8, [128, 4], [1, 8]), CT.a(bq * 8, [TS, 4], [1, 8]), [AB("CT")], [CTbb])
                    for g in range(4):
                        mm(pf[yoff_banks[g]].a(0), CTb.a(g * 128, [1, 128]), h0T.a(g * 512, [1, 512]), bq == 0, bq == 15,
                           [CTbb, h0Tb], [PS(yoff_banks[g])])
                    memset("dve", CTb.a(bq * 8, [128, 4], [1, 8]), 0.0, [CTbb])
                    return h0n, h0nb

                def seq_s2(bq, h0n, h0nb, xq, xqb):
                    hf, hfb = hfs.next()
                    for g in range(4):
                        bs_ = st_ring.next()
                        for r4 in range(4):
                            rc = g * 4 + r4
                            mm(pf[bs_].a(r4 * 128, [1, 128]), xq.a(rc * 128, [1, 128]), B_tm.a(g * 128, [1, 128]), True, True,
                               [xqb, AB("B_tm")], [PS(bs_)])
                        for r4 in range(4):
                            rc = g * 4 + r4
                            stt("dve", hf.a(rc * 128, [1, 128]), h0n.a(rc * 128, [1, 128]), cdN.a(rc * 16 + bq, [1, 1]),
                                pf[bs_].a(r4 * 128, [1, 128]), ALU.mult, ALU.add, [h0nb, AB("cdN"), PS(bs_)], [hfb])
                    P.dma("pool", bass.AP(ssm_s_d.h, bq * 2048 * 128, [[128, 128], [128 * 128, 16], [1, 128]]),
                          hf.a(0, [128, 16], [1, 128]), reads=[hfb])

                pend1 = seq_s1(0)
                for bq in range(16):
                    nxt1 = seq_s1(bq + 1) if bq < 15 else None
                    xq, xqb = xq_next
                    if bq < 15:
                        xq_next = make_xq(bq + 1)
                    seq_s2(bq, pend1[0], pend1[1], xq, xqb)
                    pend1 = nxt1
            yv, yvb = ys.next()
            POOLE = cfg.get("pool_eng", "dve")

            def stage1(hb):
                g = hb // 2
                bb = trr.next()
                for hh in range(4):
                    h = hb * 4 + hh
                    o_ = pf[bb].a(hh * 128, [1, 128])
                    for part in range(2):
                        mm(o_, ident_b.a(h, [0, 128], np_=32), abf.a(part * 128, [1, 128], np_=32), part == 0, False,
                           ["ident_b", abfb], [PS(bb)])
                    mm(o_, ident_b.a(0), maskb.a(0), False, True, ["ident_b", maskbn], [PS(bb)])
                E, Eb = Es.next()
                for hh in range(4):
                    h = hb * 4 + hh
                    act(E.a(hh * 128, [1, 128]), pf[bb].a(hh * 128, [1, 128]), AF.Exp, [PS(bb), smb], [Eb], bias=sm_.a(32 + h, [1, 1]))
                MT, MTb = MTs.next()
                stt("dve", MT.a(0, [128, 4], [1, 128]), E.a(0, [128, 4], [1, 128]), 1.0, CBm.a(g * 128, [0, 4], [1, 128]),
                    ALU.min, ALU.mult, [Eb, CBmb], [MTb])
                return MT, MTb

            def stage2(hb, MT, MTb, byd):
                for hh in range(4):
                    h = hb * 4 + hh
                    mm(pf[byd].a((h % 8) * 64, [1, 64]), MT.a(hh * 128, [1, 128]), xdt.a(h * 64, [1, 64]), True, True,
                       [MTb, xdtb], [PS(byd)])

            pend = stage1(0)
            byd = None
            for hb in range(8):
                g = hb // 2
                nxt = stage1(hb + 1) if hb < 7 else None
                if hb % 2 == 0:
                    byd = auxr.next() if prm else st_ring.next()
                stage2(hb, pend[0], pend[1], byd)
                pend = nxt
                if hb % 2 == 0:
                    continue
                if hb == 5:
                    yield "B1"
                if prm:
                    byo = auxr.next()
                    mm(pf[byo].a(0), CT.a(g * TS + j * 128, [1, 128]), hst_b.a(g * 512, [1, 512]), True, True, [AB("CT"), "hst_b"], [PS(byo)])
                else:
                    byo = yoff_banks[g]
                t1, t1b = t1s.next()
                t2, t2b = t2s.next()
                tt("dve", t1.a(0, [64, 8], [1, 64]), pf[byo].a(0, [64, 8], [1, 64]), sm_.a(2 * 32 + g * 8, [1, 8], [0, 64]), ALU.mult,
                   [PS(byo), smb], [t1b])
                tt("dve", t1.a(0), pf[byd].a(0), t1.a(0), ALU.add, [PS(byd), t1b], [t1b])
                tt(POOLE, t2.a(0, [64, 8], [1, 64]), xs_tm.a(j * 2048 + g * 512, [64, 8], [1, 64]), D_bc.a(g * 8, [1, 8], [0, 64]), ALU.mult,
                   [AB("xs_tm"), "params"], [t2b])
                tt("dve", t1.a(0), t1.a(0), t2.a(0), ALU.add, [t1b, t2b], [t1b])
                tt("dve", yv.a(g * 512, [1, 512]), t1.a(0), szs.a(g * 512, [1, 512]), ALU.mult, [t1b, szsb], [yvb])
            yield "BT"
            if prm:
                bcd = trr.next()
                mm(pf[bcd].a(0, [1, 32]), ident_f.a(127, [0, 128]), etot, True, True, ["ident_f", smb], [PS(bcd)])
                cp("dve", cdv, pf[bcd].a(0, [1, 32]), [PS(bcd)], [smb])
                for g in range(4):
                    bs_ = auxr.next()
                    mm(pf[bs_].a(0), B_tm.a(j * 512 + g * 128, [1, 128]), xdd.a(g * 512, [1, 512]), True, True, [AB("B_tm"), xddb], [PS(bs_)])
                    t2, t2b = t2s.next()
                    tt(POOLE, t2.a(0, [64, 8], [1, 64]), hst.a(g * 512, [64, 8], [1, 64]), sm_.a(7 * 32 + g * 8, [1, 8], [0, 64]), ALU.mult,
                       ["hst", smb], [t2b])
                    tt("dve", hst.a(g * 512, [1, 512]), t2.a(0), pf[bs_].a(0), ALU.add, [t2b, PS(bs_)], ["hst"])
                    cp("act", hst_b.a(g * 512, [1, 512]), hst.a(g * 512, [1, 512]), ["hst"], ["hst_b"])
            r_, rb_ = rstd_of(yv.a(0), yvb, 2048)
            yn, ynb = yns.next()
            ts(POOLE, yn.a(0), yv.a(0), r_, None, ALU.mult, None, [yvb, rb_], [ynb])
            yield "TA"
            for half in range(2):
                bt = trr.next()
                for c8 in range(8):
                    tr(pb[bt].a(c8 * 128, [1, 128]), yn.a((half * 8 + c8) * 128, [1, 128]), ident_b.a(0), [ynb, "ident_b"], [PS(bt)])
                tt("dve", ysT.a(half * 8 * TS + j * 128, [TS, 8], [1, 128]), pb[bt].a(0, [128, 8], [1, 128]),
                   snwT.a(half * 8, [1, 8], [0, 128]), ALU.mult, [PS(bt), "params"], [AB("ysT")])
        def run_until(g_, marks):
            while True:
                try:
                    r = next(g_)
                except StopIteration:
                    return None
                if r in marks:
                    return r

        gens = [chunk(j) for j in range(NCH)]
        run_until(gens[0], ("HB",))
        for j in range(NCH):
            run_until(gens[j], ("B1",))
            if j > 0:
                run_until(gens[j - 1], ())
            run_until(gens[j], ("BT",))
            if j + 1 < NCH:
                run_until(gens[j + 1], ("HB",))
            run_until(gens[j], ("TA",))
        run_until(gens[NCH - 1], ())
        if nxt_tile is not None and cfg.get("prenorm", True):
            phaseA_norm(*nxt_tile)
        gate_proj(1, ysT, AB("ysT"), 16, w_so_d, False, True, gts, tmps)

        if cfg.get("nphase", 9) < 5:
            return
        phase()
        wo = [load_slab(w_o_d, 1024, 0, cb * 512) for cb in range(2)]
        if last_p:
            hfin = al("hfin", 2048, F32)
            for q4 in range(4):
                bt = trr.next()
                for i in range(4):
                    rc = q4 * 4 + i
                    tr(pf[bt].a(i * 128, [1, 128]), hst.a(rc * 128, [1, 128]), ident_f.a(0), ["hst", "ident_f"], [PS(bt)])
                cp("act", hfin.a(q4 * 512, [1, 512]), pf[bt].a(0), [PS(bt)], [AB("hfin")])
            P.dma("pool", bass.AP(ssm_p_d.h, 0, [[128, 128], [128 * 128, 16], [1, 128]]), hfin.a(0, [128, 16], [1, 128]), reads=[AB("hfin")])
        fnw_bc = al("fnw_bc", 1024, F32)
        P.dma("sp", fnw_bc.a(0), bass.AP(fnw_d.h, 0, [[0, 128], [1, 1024]]), writes=[AB("fnw_bc")])
        xrs = ring_alloc("xr", 1024, F32, 4)
        xos = ring_alloc("xo", 1024, F32, 2)
        yos = ring_alloc("yo", 1024, F32, 2)
        for j in range(NCH):
            xr, xrb = xrs.next()
            P.dma("sp", xr.a(0), x_d.a((row0 + j * 128) * 1024, [1, 1024]), writes=[xrb])
            xo, xob = xos.next()
            for cb in range(2):
                b = mmr.next()
                for k in range(8):
                    mm(pf[b].a(0), mT.a(k * 512 + j * 128, [1, 128]), wo[cb][0].a(k * 512, [1, 512]), k == 0, k == 7, ["mT", wo[cb][1][k]], [PS(b)])
                tt("dve", xo.a(cb * 512, [1, 512]), pf[b].a(0), xr.a(cb * 512, [1, 512]), ALU.add, [PS(b), xrb], [xob])
            r_, rb_ = rstd_of(xo.a(0), xob, 1024)
            yo, yob = yos.next()
            stt("dve", yo.a(0), xo.a(0), r_, fnw_bc.a(0), ALU.mult, ALU.mult, [xob, rb_, AB("fnw_bc")], [yob])
            P.dma("pool", y_d.a((row0 + j * 128) * 1024, [1, 1024]), yo.a(0), reads=[yob])

    for i_t, (kind, ti) in enumerate(tiles):
        tile(kind, ti, tiles[i_t + 1] if i_t + 1 < len(tiles) else None)
    stats = P.emit()
    if cfg.get("verbose"):
        print("ops/waits per engine:", stats, "sbuf peak", sb.peak)
    return nc, dbg_outs


def _fm(v, nchunk):
    return np.ascontiguousarray(np.asarray(v, np.float32).reshape(nchunk, 128).T)


def make_in_maps(inputs, ncores=8):
    f = lambda a: np.ascontiguousarray(np.asarray(a, np.float32))
    conv_w = f(inputs["conv_w"])[0]
    cwT = np.ascontiguousarray(conv_w.reshape(4, 24, 128).transpose(2, 1, 0).reshape(128, 96))
    shared = {
        "w_in": f(inputs["w_in"])[0], "w_grp": f(inputs["w_pool_grp"])[0].reshape(1024, 256),
        "w_mk": f(inputs["w_mem_k"])[0], "w_mv": f(inputs["w_mem_v"])[0],
        "w_po": f(inputs["w_pool_out"])[0], "w_so": f(inputs["w_ssd_out"])[0],
        "w_ao": f(inputs["w_att_out"])[0], "w_o": f(inputs["w_out"])[0],
        "nwT": _fm(inputs["norm_w"][0], 8), "mnwT": _fm(inputs["mem_norm_w"][0], 8), "pscT": _fm(inputs["pool_scale"][0], 8),
        "cwT": cwT, "cbT": _fm(inputs["conv_b"][0], 24), "snwT": _fm(inputs["ssd_norm_w"][0], 16),
        "fnw": f(inputs["final_norm_w"]).reshape(1, 1024), "dtb": f(inputs["dt_bias"]).reshape(1, 32),
        "alog": f(inputs["a_log"]).reshape(1, 32), "dsk": f(inputs["d_skip"]).reshape(1, 32),
    }
    maps = []
    for c in range(ncores):
        s = slice(16 * c, 16 * c + 16)
        m = dict(shared)
        m.update({
            "xp": f(inputs["x_prompt"][c]), "xs": f(inputs["x_sample"][s]).reshape(128, 1024),
            "mem": f(inputs["mem_prompt"][c]),
            "stp": f(inputs["state_pool"][0, s]).reshape(240, 1024), "stc": f(inputs["state_conv"][0, s]).reshape(48, 3072),
            "sts": f(inputs["state_ssm"][0, s]).reshape(32768, 128),
            "ck": f(inputs["cache_mem_k"][0, s]).reshape(4096, 1024), "cv": f(inputs["cache_mem_v"][0, s]).reshape(4096, 1024),
        })
        maps.append(m)
    return maps


_NC_CACHE = {}


def kernel(**inputs):
    if "nc" not in _NC_CACHE:
        _NC_CACHE["nc"] = build()[0]
    nc = _NC_CACHE["nc"]
    maps = make_in_maps(inputs)
    res = run_bass_kernel_spmd(nc, maps, core_ids=list(range(8)))
    R = res.results
    cat = lambda k: np.stack([np.asarray(r[k], np.float32) for r in R])
    y_p = cat("y_p")
    y_s = cat("y_s").reshape(128, 8, 1024)
    pool_p = cat("pool_p")[None]
    conv_p = cat("conv_p")[None]
    ssm_p = cat("ssm_p").reshape(1, 8, 4, 8, 64, 128)
    mk_p = cat("mk_p").reshape(1, 8, 256, 4, 256)
    mv_p = cat("mv_p").reshape(1, 8, 256, 4, 256)
    pool_s = cat("pool_s").reshape(1, 128, 15, 1024)
    conv_s = cat("conv_s").reshape(1, 128, 3, 3072)
    ssm_s = cat("ssm_s").reshape(1, 128, 4, 8, 64, 128)
    return (y_p, y_s, pool_p, conv_p, ssm_p, mk_p, mv_p, pool_s, conv_s, ssm_s)
```

```python
import numpy as np
import concourse.bass as bass
import concourse.mybir as mybir
from concourse.bass_utils import run_bass_kernel_spmd

F32 = mybir.dt.float32
BF16 = mybir.dt.bfloat16
AF = mybir.ActivationFunctionType
ALU = mybir.AluOpType
AX = mybir.AxisListType

ENGINES = ("pe", "act", "dve", "pool", "sp")
NDMA_SEM = 16
EPS = 1e-6

D = 1024
IN_COLS = 12320
OFF_U, OFF_ZP, OFF_ZS, OFF_XBC, OFF_DT, OFF_Q, OFF_ZA, OFF_G = 0, 1024, 2048, 4096, 7168, 7200, 8224, 9248


class Buf:
    __slots__ = ("name", "last_w", "readers", "dma_readers", "arena", "inherit", "claimed")

    def __init__(self, name, arena=False):
        self.name = name
        self.last_w = None
        self.readers = {}
        self.dma_readers = []
        self.arena = arena
        self.inherit = None
        self.claimed = False


class Op:
    __slots__ = ("eng", "fn", "deps", "is_dma", "sem", "semval", "signals", "sig", "prev_same_sem")

    def __init__(self, eng, fn, is_dma=False):
        self.eng = eng
        self.fn = fn
        self.deps = set()
        self.is_dma = is_dma
        self.sem = None
        self.semval = 0
        self.signals = False
        self.sig = 0
        self.prev_same_sem = None


class Prog:
    def __init__(self, nc):
        self.nc = nc
        self.streams = {e: [] for e in ENGINES}
        self.ndma = {e: 0 for e in ENGINES}
        self.dma_last = {}
        self.bufs = {}
        self.cur_barrier = None
        self.arena_dmas = []
        self.bar_t = None
        self.live = []

    def claim(self, b, lo, hi):
        inh = set()
        keep = []
        for (l2, h2, b2) in self.live:
            if l2 < hi and lo < h2 and b2 is not b:
                if b2.last_w is not None:
                    inh.add(b2.last_w)
                inh.update(b2.readers.values())
                inh.update(b2.dma_readers)
                if b2.inherit:
                    inh |= b2.inherit
                if not (lo <= l2 and h2 <= hi):
                    keep.append((l2, h2, b2))
            else:
                keep.append((l2, h2, b2))
        keep.append((lo, hi, b))
        self.live = keep
        b.inherit = (b.inherit or set()) | inh
        b.claimed = True

    def buf(self, name, arena=False):
        b = self.bufs.get(name)
        if b is None:
            b = self.bufs[name] = Buf(name, arena)
        return b

    def _track(self, o, reads, writes):
        rb = [self.buf(b) if not isinstance(b, Buf) else b for b in reads]
        wb = [self.buf(b) if not isinstance(b, Buf) else b for b in writes]
        ex = [b for b in rb if isinstance(b.name, tuple) and b.name[0] == "ps" and b not in wb]
        if ex:
            rb = [b for b in rb if b not in ex]
            wb = wb + ex
        arena = False
        for b in rb + wb:
            if b.arena:
                assert b.claimed, ("unclaimed arena buffer", b.name)
            if b.inherit:
                o.deps |= b.inherit
        for b in wb:
            b.inherit = None
        for b in rb:
            arena = arena or b.arena
            if b.last_w is not None:
                o.deps.add(b.last_w)
        for b in wb:
            arena = arena or b.arena
            if b.last_w is not None:
                o.deps.add(b.last_w)
            o.deps.update(b.readers.values())
            o.deps.update(b.dma_readers)
        for b in rb:
            if o.is_dma:
                b.dma_readers.append(o)
            else:
                b.readers[o.eng] = o
        for b in wb:
            b.last_w = o
            b.readers = {}
            b.dma_readers = []
        if arena and self.cur_barrier is not None:
            o.deps.add(self.cur_barrier)
        if arena and o.is_dma:
            self.arena_dmas.append(o)
        o.deps.discard(o)

    def op(self, eng, fn, reads=(), writes=()):
        o = Op(eng, fn)
        self._track(o, reads, writes)
        if eng == "pe":
            o.deps = {d for d in o.deps if d.is_dma or d.eng != "pe"}
        self.streams[eng].append(o)
        return o

    def dma(self, eng, out, in_, reads=(), writes=()):
        o = Op(eng, lambda e: e.dma_start(out=out, in_=in_), is_dma=True)
        self._track(o, reads, writes)
        slot = self.ndma[eng] % NDMA_SEM
        self.ndma[eng] += 1
        prev = self.dma_last.get((eng, slot))
        o.prev_same_sem = prev
        o.sem = (eng, slot)
        o.semval = (prev.semval if prev is not None else 0) + 16
        self.dma_last[(eng, slot)] = o
        self.streams[eng].append(o)
        return o

    def barrier(self):
        t = self.bar_t
        o = Op("dve", lambda e: e.memset(t[:, 0:1], 0.0))
        for e in ENGINES:
            for p in reversed(self.streams[e]):
                if not p.is_dma:
                    o.deps.add(p)
                    break
        o.deps.update(self.arena_dmas)
        self.arena_dmas = []
        self.streams["dve"].append(o)
        self.cur_barrier = o
        return o

    def emit(self):
        nc = self.nc
        for e in ENGINES:
            for o in self.streams[e]:
                for d in o.deps:
                    if not d.is_dma:
                        d.signals = True
        for e in ENGINES:
            c = 0
            for o in self.streams[e]:
                if not o.is_dma and o.signals:
                    c += 1
                    o.sig = c
        esem = {e: nc.alloc_semaphore("es_" + e) for e in ENGINES}
        dsem = {}
        for e in ENGINES:
            for s in range(min(NDMA_SEM, self.ndma[e])):
                dsem[(e, s)] = nc.alloc_semaphore("ds_%s_%d" % (e, s))
        streams = self.streams
        dma_last = self.dma_last
        stats = {}

        def run(ename, eng):
            waited = {}
            nw = 0

            def wait(key, sem, val):
                nonlocal nw
                if waited.get(key, 0) >= val:
                    return
                eng.wait_ge(sem, val)
                waited[key] = val
                nw += 1

            for o in streams[ename]:
                if o.is_dma and o.prev_same_sem is not None:
                    p = o.prev_same_sem
                    wait(("d",) + p.sem, dsem[p.sem], p.semval)
                for d in sorted(o.deps, key=lambda d: (d.is_dma, d.eng, d.sig, d.semval)):
                    if d.is_dma:
                        wait(("d",) + d.sem, dsem[d.sem], d.semval)
                    else:
                        wait(("e", d.eng), esem[d.eng], d.sig)
                ins = o.fn(eng)
                if o.is_dma:
                    ins.then_inc(dsem[o.sem], 16)
                elif o.signals:
                    ins.then_inc(esem[ename], 1)
            for (e, s), o in dma_last.items():
                if e == ename:
                    wait(("d", e, s), dsem[(e, s)], o.semval)
            stats[ename] = (len(streams[ename]), nw)

        with nc.Block() as block:
            @block.sync
            def _(e):
                run("sp", e)

            @block.scalar
            def _(e):
                run("act", e)

            @block.vector
            def _(e):
                run("dve", e)

            @block.gpsimd
            def _(e):
                run("pool", e)

            @block.tensor
            def _(e):
                run("pe", e)
        return stats


class TT:
    def __init__(self, h, F):
        self.h = h
        self.F = F

    def a(self, off=0, *dims, p0=0, np_=128):
        if not dims:
            dims = ([1, self.F - off],)
        return bass.AP(self.h, p0 * self.F + off, [[self.F, np_]] + [list(d) for d in dims])


class SB:
    def __init__(self, nc):
        self.nc = nc
        self.ptr = 16512
        self.top = 229344
        self.n = 0
        self.peak = 0

    def alloc(self, name, cols, dtype):
        sz = 2 if dtype == BF16 else 4
        off = (self.ptr + 31) // 32 * 32
        self.ptr = off + cols * sz
        self.peak = max(self.peak, self.ptr)
        assert self.ptr <= self.top, ("SBUF overflow", name, self.ptr)
        self.n += 1
        h = self.nc.alloc_sbuf_tensor_at("%s_%d" % (name, self.n), [128, cols], dtype, offset=off)
        t = TT(h, cols)
        t.lo, t.hi, t.esz = off, off + cols * sz, sz
        return t

    def mark(self):
        return self.ptr

    def reset(self, m):
        self.ptr = m


class Ring:
    def __init__(self, items):
        self.items = items
        self.i = 0

    def next(self):
        r = self.items[self.i % len(self.items)]
        self.i += 1
        return r


def build(cfg=None):
    cfg = cfg or {}
    tiles = cfg.get("tiles", [("p", 0), ("p", 1), ("p", 2), ("p", 3), ("s", 0)])
    do_mem = cfg.get("mem", True)
    nc = bass.Bass("TRN2", target_bir_lowering=False)
    P = Prog(nc)
    sb = SB(nc)

    def din(name, rows, cols):
        return TT(nc.dram_tensor(name, [rows, cols], F32, kind="ExternalInput"), cols)

    def dout(name, rows, cols):
        return TT(nc.dram_tensor(name, [rows, cols], F32, kind="ExternalOutput"), cols)

    xp_d, xs_d, mem_d = din("xp", 2048, 1024), din("xs", 128, 1024), din("mem", 256, 1024)
    stp_d, stc_d, sts_d = din("stp", 240, 1024), din("stc", 48, 3072), din("sts", 32768, 128)
    ck_d, cv_d = din("ck", 4096, 1024), din("cv", 4096, 1024)
    w_in_d, w_grp_d = din("w_in", 1024, IN_COLS), din("w_grp", 1024, 256)
    w_mk_d, w_mv_d = din("w_mk", 1024, 1024), din("w_mv", 1024, 1024)
    w_po_d, w_so_d, w_ao_d, w_o_d = din("w_po", 1024, 1024), din("w_so", 2048, 1024), din("w_ao", 1024, 1024), din("w_o", 1024, 1024)
    nwT_d, mnwT_d, pscT_d = din("nwT", 128, 8), din("mnwT", 128, 8), din("pscT", 128, 8)
    cwT_d, cbT_d, snwT_d = din("cwT", 128, 96), din("cbT", 128, 24), din("snwT", 128, 16)
    fnw_d, dtb_d, alog_d, dsk_d = din("fnw", 1, 1024), din("dtb", 1, 32), din("alog", 1, 32), din("dsk", 1, 32)

    y_p_d, y_s_d = dout("y_p", 2048, 1024), dout("y_s", 128, 1024)
    pool_p_d, conv_p_d, ssm_p_d = dout("pool_p", 15, 1024), dout("conv_p", 3, 3072), dout("ssm_p", 2048, 128)
    mk_p_d, mv_p_d = dout("mk_p", 256, 1024), dout("mv_p", 256, 1024)
    pool_s_d, conv_s_d, ssm_s_d = dout("pool_s", 240, 1024), dout("conv_s", 48, 3072), dout("ssm_s", 32768, 128)
    dbg_outs = {}

    pf, pb = [], []
    for i in range(8):
        h = nc.alloc_psum_tensor("psb%d" % i, [128, 512], F32)
        pf.append(TT(h, 512))
        pb.append(TT(h.bitcast(BF16), 1024))
    PS = lambda i: ("ps", i)
    mmr, trr, auxr = Ring([0, 1, 2]), Ring([3, 4]), Ring([5, 6, 7])

    ident_f, ident_b = sb.alloc("ident_f", 128, F32), sb.alloc("ident_b", 128, BF16)
    tri_p, tri_s, ones_f, S_s = sb.alloc("tri_p", 128, F32), sb.alloc("tri_s", 128, F32), sb.alloc("ones_f", 128, F32), sb.alloc("S_s", 128, F32)
    seqmask = sb.alloc("seqmask", 16, F32)
    maskb_p, maskb_s = sb.alloc("maskb_p", 128, BF16), sb.alloc("maskb_s", 128, BF16)
    nwT, mnwT, pscT = sb.alloc("nwT", 8, F32), sb.alloc("mnwT", 8, F32), sb.alloc("pscT", 8, F32)
    cwT, cbT, snwT = sb.alloc("cwT", 96, F32), sb.alloc("cbT", 24, F32), sb.alloc("snwT", 16, F32)
    dtb_bc, A_bc, D_bc = sb.alloc("dtb_bc", 32, F32), sb.alloc("A_bc", 32, F32), sb.alloc("D_bc", 32, F32)
    wgrp = sb.alloc("wgrp", 2048, BF16)
    bar_t = sb.alloc("bar_t", 8, F32)
    P.bar_t = bar_t.h
    junk = sb.alloc("junk", 2048, BF16)
    st = sb.alloc("st", 64, F32)
    eps_t = sb.alloc("eps_t", 8, F32)
    hT = sb.alloc("hT", 8 * 512, BF16)
    NSLAB = 4
    slabs = [sb.alloc("slab%d" % i, 4096, BF16) for i in range(NSLAB)]
    slabr = Ring(list(range(NSLAB)))
    merged = sb.alloc("merged", 8 * 512, F32)
    mT = sb.alloc("mT", 8 * 512, BF16)
    ARENA0_S = sb.mark()
    kT = sb.alloc("kT", 8 * 256, BF16)
    v_b = sb.alloc("v_b", 2 * 1024, BF16)
    hst = sb.alloc("hst", 2048, F32)
    hst_b = sb.alloc("hst_b", 2048, BF16)
    phist = sb.alloc("phist", 8 * 15, F32)
    chist = sb.alloc("chist", 24 * 3, F32)
    ARENA0_P = sb.mark()
    for nm_, t_ in (("kT", kT), ("v_b", v_b), ("hst", hst), ("hst_b", hst_b), ("phist", phist), ("chist", chist)):
        P.claim(P.buf(nm_), t_.lo, t_.hi)

    def ap(t, off, *dims, p0=0, np_=128):
        return t.a(off, *dims, p0=p0, np_=np_)

    def dbg(name, t, cols, reads):
        if not cfg.get("dbg"):
            return
        d = TT(nc.dram_tensor("dbg_" + name, [128, cols], F32, kind="ExternalOutput"), cols)
        stg = sb.alloc("dbgs_" + name, cols, F32)
        P.op("dve", lambda e: e.tensor_copy(out=stg.a(0), in_=t), reads=reads, writes=[P.buf("dbgs_" + name, True)])
        P.dma("sp", d.a(0), stg.a(0), reads=[P.buf("dbgs_" + name, True)])
        dbg_outs[name] = cols

    def mm(out, lhsT, rhs, start, stop, reads, writes, skip=False):
        if skip:
            P.op("pe", lambda e: e.matmul(out, lhsT=lhsT, rhs=rhs, start=start, stop=stop, skip_group_check=True), reads, writes)
        else:
            P.op("pe", lambda e: e.matmul(out, lhsT=lhsT, rhs=rhs, start=start, stop=stop), reads, writes)

    def tr(out, in_, ident, reads, writes):
        P.op("pe", lambda e: e.transpose(out, in_, ident), reads, writes)

    def act(out, in_, func, reads, writes, bias=None, scale=None, accum=None):
        kw = {}
        if bias is not None:
            kw["bias"] = bias
        if scale is not None:
            kw["scale"] = scale
        if accum is not None:
            kw["accum_out"] = accum
        P.op("act", lambda e: e.activation(out=out, in_=in_, func=func, **kw), reads, writes)

    def tt(eng, out, in0, in1, op, reads, writes):
        P.op(eng, lambda e: e.tensor_tensor(out=out, in0=in0, in1=in1, op=op), reads, writes)

    def ts(eng, out, in0, s1, s2, op0, op1, reads, writes):
        if s2 is None:
            P.op(eng, lambda e: e.tensor_scalar(out=out, in0=in0, scalar1=s1, scalar2=None, op0=op0), reads, writes)
        else:
            P.op(eng, lambda e: e.tensor_scalar(out=out, in0=in0, scalar1=s1, scalar2=s2, op0=op0, op1=op1), reads, writes)

    def stt(eng, out, in0, scalar, in1, op0, op1, reads, writes):
        P.op(eng, lambda e: e.scalar_tensor_tensor(out=out, in0=in0, scalar=scalar, in1=in1, op0=op0, op1=op1), reads, writes)

    def cp(eng, out, in_, reads, writes):
        if eng == "act":
            P.op("act", lambda e: e.activation(out=out, in_=in_, func=AF.Copy), reads, writes)
        else:
            P.op(eng, lambda e: e.tensor_copy(out=out, in_=in_), reads, writes)

    def memset(eng, out, val, writes, reads=()):
        P.op(eng, lambda e: e.memset(out, val), reads, writes)

    def asel(out, in_, pattern, cmp_, fill, base, cm, reads, writes):
        P.op("pool", lambda e: e.affine_select(out=out, in_=in_, pattern=pattern, compare_op=cmp_, fill=fill, base=base,
                                               channel_multiplier=cm), reads, writes)

    def recip(out, in_, reads, writes):
        P.op("dve", lambda e: e.reciprocal(out=out, in_=in_), reads, writes)

    def load_slab(dr, ncols_total, row0, col0, nk=8, ncols=512):
        i = slabr.next()
        if nk == 8 and ncols == 512 and cfg.get("split_slab", True):
            bufs = []
            for hf_ in range(2):
                o = slabs[i].a(hf_ * 4 * 512, [512, 4], [1, ncols])
                src = bass.AP(dr.h, (row0 + hf_ * 512) * ncols_total + col0, [[ncols_total, 128], [128 * ncols_total, 4], [1, ncols]])
                P.dma("pool", o, src, writes=[("slab", i, hf_)])
                bufs += [("slab", i, hf_)] * 4
            return slabs[i], bufs
        o = slabs[i].a(0, [512, nk], [1, ncols])
        src = bass.AP(dr.h, row0 * ncols_total + col0, [[ncols_total, 128], [128 * ncols_total, nk], [1, ncols]])
        P.dma("pool", o, src, writes=[("slab", i, 0), ("slab", i, 1)])
        return slabs[i], [("slab", i, 0)] * 4 + [("slab", i, 1)] * 4

    memset("pool", ident_f.a(0), 0.0, ["ident_f"])
    asel(ident_f.a(0), ident_f.a(0), [[-1, 128]], ALU.not_equal, 1.0, 0, 1, ["ident_f"], ["ident_f"])
    cp("pool", ident_b.a(0), ident_f.a(0), ["ident_f"], ["ident_b"])
    memset("pool", ones_f.a(0), 1.0, ["ones_f"])
    memset("pool", tri_p.a(0), 1.0, ["tri_p"])
    asel(tri_p.a(0), tri_p.a(0), [[1, 128]], ALU.is_ge, 0.0, 0, -1, ["tri_p"], ["tri_p"])
    memset("pool", S_s.a(0), 1.0, ["S_s"])
    asel(S_s.a(0, [8, 16], [1, 8]), S_s.a(0, [8, 16], [1, 8]), [[-8, 16], [0, 8]], ALU.is_ge, 0.0, 0, 1, ["S_s"], ["S_s"])
    asel(S_s.a(0, [8, 16], [1, 8]), S_s.a(0, [8, 16], [1, 8]), [[8, 16], [0, 8]], ALU.is_ge, 0.0, 7, -1, ["S_s"], ["S_s"])
    asel(tri_s.a(0), S_s.a(0), [[1, 128]], ALU.is_ge, 0.0, 0, -1, ["S_s"], ["tri_s"])
    ts("pool", maskb_p.a(0), tri_p.a(0), -1.0, 30000.0, ALU.add, ALU.mult, ["tri_p"], ["maskb_p"])
    ts("pool", maskb_s.a(0), tri_s.a(0), -1.0, 30000.0, ALU.add, ALU.mult, ["tri_s"], ["maskb_s"])
    memset("pool", seqmask.a(0), 1.0, ["seqmask"])
    asel(seqmask.a(0), seqmask.a(0), [[-8, 16]], ALU.is_ge, 0.0, 0, 1, ["seqmask"], ["seqmask"])
    asel(seqmask.a(0), seqmask.a(0), [[8, 16]], ALU.is_ge, 0.0, 7, -1, ["seqmask"], ["seqmask"])
    memset("pool", phist.a(0), 0.0, ["phist"])
    memset("pool", chist.a(0), 0.0, ["chist"])
    memset("pool", hst.a(0), 0.0, ["hst"])
    memset("pool", hst_b.a(0), 0.0, ["hst_b"])
    pnames = []
    for t_, d_ in ((nwT, nwT_d), (mnwT, mnwT_d), (pscT, pscT_d), (cwT, cwT_d), (cbT, cbT_d), (snwT, snwT_d)):
        pnames.append("params%d" % len(pnames))
        P.dma("sp", t_.a(0), d_.a(0), writes=[pnames[-1]])
    for t_, d_, n_ in ((dtb_bc, dtb_d, 32), (A_bc, alog_d, 32), (D_bc, dsk_d, 32)):
        pnames.append("params%d" % len(pnames))
        P.dma("sp", t_.a(0), bass.AP(d_.h, 0, [[0, 128], [1, n_]]), writes=[pnames[-1]])
    memset("dve", bar_t.a(4, [1, 1]), 0.0, ["params"], reads=pnames)
    act(A_bc.a(0), A_bc.a(0), AF.Exp, ["params"], ["params"])
    ts("dve", A_bc.a(0), A_bc.a(0), -1.0, None, ALU.mult, None, ["params"], ["params"])
    P.dma("pool", wgrp.a(0, [256, 8], [1, 256]), bass.AP(w_grp_d.h, 0, [[256, 128], [128 * 256, 8], [1, 256]]), writes=["wgrp"])

    strr = Ring([0, 1, 2, 3])

    def rstd_of(xin, xin_b, width):
        i = strr.next()
        bs_ = ("st", i)
        act(junk.a(0, [1, width]), xin, AF.Square, [xin_b], ["junk", bs_], accum=st.a(i * 4, [1, 1]))
        ts("dve", st.a(i * 4 + 1, [1, 1]), st.a(i * 4, [1, 1]), 1.0 / width, EPS, ALU.mult, ALU.add, [bs_], [bs_])
        act(st.a(i * 4 + 1, [1, 1]), st.a(i * 4 + 1, [1, 1]), AF.Ln, [bs_], [bs_])
        act(st.a(i * 4 + 1, [1, 1]), st.a(i * 4 + 1, [1, 1]), AF.Exp, [bs_], [bs_], scale=-0.5)
        return st.a(i * 4 + 1, [1, 1]), bs_

    def rms_rows(xin, xin_b, xout, xout_b, width):
        r, rb_ = rstd_of(xin, xin_b, width)
        ts("dve", xout, xin, r, None, ALU.mult, None, [xin_b, rb_], [xout_b])

    memset("pool", eps_t.a(0), EPS, ["eps_t"])
    EPS_AP = eps_t.a(0, [1, 1])

    def norm_transpose(x_rows_ap, dstT, dst_buf, col0, wT, T_, xt, xn, bx, bn):
        P.dma("sp", xt.a(0), x_rows_ap, writes=[bx])
        rms_rows(xt.a(0), bx, xn.a(0), bn, 1024)
        for half in range(2):
            b = trr.next()
            for c4 in range(4):
                c = half * 4 + c4
                tr(pf[b].a(c4 * 128, [1, 128]), xn.a(c * 128, [1, 128]), ident_f.a(0), [bn, "ident_f"], [PS(b)])
            tt("dve", dstT.a(half * 4 * T_ + col0, [T_, 4], [1, 128]), pf[b].a(0, [128, 4], [1, 128]),
               wT.a(half * 4, [1, 4], [0, 128]), ALU.mult, [PS(b), "params"], [dst_buf])

    def transpose_to(xn, bn, dstT, dst_buf, col0, wT, T_):
        for half in range(2):
            b = trr.next()
            for c4 in range(4):
                c = half * 4 + c4
                tr(pf[b].a(c4 * 128, [1, 128]), xn.a(c * 128, [1, 128]), ident_f.a(0), [bn, "ident_f"], [PS(b)])
            tt("dve", dstT.a(half * 4 * T_ + col0, [T_, 4], [1, 128]), pf[b].a(0, [128, 4], [1, 128]),
               wT.a(half * 4, [1, 4], [0, 128]), ALU.mult, [PS(b), "params"], [dst_buf])

    prenorm = {}
    PRE_BASE = None

    def phaseA_norm(kind2, ti2):
        prm2 = kind2 == "p"
        nch2 = 4 if prm2 else 1
        x_d2 = xp_d if prm2 else xs_d
        row02 = ti2 * 512 if prm2 else 0
        save = sb.ptr
        sb.ptr = ARENA0_P + 72 * 1024
        tg = (kind2, ti2)

        def alp(name, bn):
            t = sb.alloc(name, 1024, F32)
            bb_ = P.buf((tg, bn), True)
            P.claim(bb_, t.lo, t.hi)
            return t, bb_
        xts_ = [alp("xt", ("xt", i_)) for i_ in range(2)]
        outs = []
        for j in range(nch2):
            xn_, bn_ = alp("xn", ("xn", j))
            xt_, bx_ = xts_[j % 2]
            P.dma("sp", xt_.a(0), x_d2.a((row02 + j * 128) * 1024, [1, 1024]), writes=[bx_])
            rms_rows(xt_.a(0), bx_, xn_.a(0), bn_, 1024)
            outs.append((xn_, bn_))
        sb.ptr = save
        prenorm[tg] = outs

    if cfg.get("stop") == "const":
        do_mem = False
    if do_mem:
        m0 = ARENA0_P
        sb.reset(m0)
        Bm = lambda n: P.buf(("mem", n), True)

        def alm(name, cols, dtype, bn=None):
            t = sb.alloc(name, cols, dtype)
            P.claim(Bm(bn if bn is not None else name), t.lo, t.hi)
            return t
        mhT = alm("mhT", 8 * 256, BF16)
        kvf = alm("kvf", 2 * 1024, F32)
        k_bf = alm("k_bf", 2 * 1024, BF16)
        for mc in range(2):
            xt_, xn_ = alm("xt", 1024, F32, ("xt", mc)), alm("xn", 1024, F32, ("xn", mc))
            norm_transpose(mem_d.a(mc * 128 * 1024, [1, 1024]), mhT, Bm("mhT"), mc * 128, mnwT, 256, xt_, xn_,
                           Bm(("xt", mc)), Bm(("xn", mc)))
        for wi, (W, od) in enumerate(((w_mk_d, mk_p_d), (w_mv_d, mv_p_d)) if cfg.get("stop") != "memnorm" else ()):
            for cb in range(2):
                sl, slb = load_slab(W, 1024, 0, cb * 512)
                for mc in range(2):
                    b = mmr.next()
                    for k in range(8):
                        mm(pf[b].a(0), mhT.a(k * 256 + mc * 128, [1, 128]), sl.a(k * 512, [1, 512]), k == 0, k == 7,
                           [Bm("mhT"), slb[k]], [PS(b)])
                    cp("act", kvf.a(mc * 1024 + cb * 512, [1, 512]), pf[b].a(0), [PS(b)], [Bm("kvf")])
                    dst = k_bf if wi == 0 else v_b
                    dstb = Bm("k_bf") if wi == 0 else "v_b"
                    cp("dve", dst.a(mc * 1024 + cb * 512, [1, 512]), pf[b].a(0), [PS(b)], [dstb])
            for mc in range(2):
                P.dma("pool", od.a(mc * 128 * 1024, [1, 1024]), kvf.a(mc * 1024, [1, 1024]), reads=[Bm("kvf")])
        for mc in range(2 if cfg.get("stop") != "memnorm" else 0):
            b = trr.next()
            for i in range(8):
                tr(pb[b].a(i * 128, [1, 128]), k_bf.a(mc * 1024 + i * 128, [1, 128]), ident_b.a(0), [Bm("k_bf"), "ident_b"], [PS(b)])
            cp("act", kT.a(mc * 128, [256, 8], [1, 128]), pb[b].a(0, [128, 8], [1, 128]), [PS(b)], ["kT"])
        sb.reset(m0)

    def fm(sl, slb, e4, T_, extra_reads=()):
        b = mmr.next()
        for k in range(8):
            mm(pf[b].a(0, [1, T_]), sl.a(k * 512 + e4 * 128, [1, 128]), hT.a(k * 512, [1, T_]), k == 0, k == 7,
               [slb[k], "hT"], [PS(b)])
        return b

    def tmm(sl, slb, j, ncols=512):
        b = mmr.next()
        for k in range(8):
            mm(pf[b].a(0, [1, ncols]), hT.a(k * 512 + j * 128, [1, 128]), sl.a(k * 512, [1, ncols]), k == 0, k == 7,
               [slb[k], "hT"], [PS(b)])
        return b

    def tile(kind, ti, nxt_tile=None):
        prm = kind == "p"
        T_ = 512 if prm else 128
        NCH = T_ // 128
        NS, L = (1, 512) if prm else (16, 8)
        tag = (kind, ti)
        x_d = xp_d if prm else xs_d
        y_d = y_p_d if prm else y_s_d
        row0 = ti * 512 if prm else 0
        last_p = prm and ti == 3
        need_tm = last_p or not prm
        jl = NCH - 1
        TS = T_

        def AB(n):
            return P.buf((tag, n), True)

        def phase():
            sb.reset(ARENA0_P if prm else ARENA0_S)

        def al(name, cols, dtype, bn=None, parts=None):
            t = sb.alloc(name, cols, dtype)
            if parts is None:
                P.claim(AB(bn if bn is not None else name), t.lo, t.hi)
            else:
                for (bname, off, n) in parts:
                    P.claim(AB(bname), t.lo + off * t.esz, t.lo + (off + n) * t.esz)
            return t

        def ring_alloc(name, cols, dtype, n):
            return Ring([(al(name, cols, dtype, bn=(name, i)), AB((name, i))) for i in range(n)])

        phase()
        if prm:
            sb.ptr = ARENA0_P + 72 * 1024
        if tag in prenorm:
            for j, (xn_, bn_) in enumerate(prenorm[tag]):
                transpose_to(xn_, bn_, hT, "hT", j * 128, nwT, 512)
        else:
            xts = [(al("xt", 1024, F32, bn=("xt", i_)), al("xn", 1024, F32, bn=("xn", i_))) for i_ in range(2)]
            for j in range(NCH):
                xt_, xn_ = xts[j % 2]
                norm_transpose(x_d.a((row0 + j * 128) * 1024, [1, 1024]), hT, "hT", j * 128, nwT, 512, xt_, xn_,
                               AB(("xt", j % 2)), AB(("xn", j % 2)))

        def tm_rows(sl, slb, stage_ring, emit_rows):
            b = tmm(sl, slb, jl)
            stg, stgb = stage_ring.next()
            cp("act", stg.a(0), pf[b].a(0), [PS(b)], [stgb])
            emit_rows(stg, stgb)

        def gate_proj(bidx, yT, yTb, nk, w_d, first, last, gts, tmps):
            for half in range(2):
                gs, gsb = load_slab(w_in_d, IN_COLS, 0, OFF_G + bidx * 1024 + half * 512)
                wos = [load_slab(w_d, 1024, kk * 1024, half * 512) for kk in range(nk // 8)]
                for c4 in range(4):
                    c = half * 4 + c4
                    bg = fm(gs, gsb, c4, T_)
                    gt, gtb = gts.next()
                    act(gt.a(0, [1, T_]), pf[bg].a(0, [1, T_]), AF.Sigmoid, [PS(bg)], [gtb])
                    bp = mmr.next()
                    for k in range(nk):
                        wsl, wslb = wos[k // 8]
                        mm(pf[bp].a(0, [1, T_]), wsl.a((k % 8) * 512 + c4 * 128, [1, 128]), yT.a(k * TS, [1, T_]),
                           k == 0, k == nk - 1, [wslb[k % 8], yTb], [PS(bp)])
                    mg = merged.a(c * 512, [1, T_])
                    if first:
                        tt("dve", mg, pf[bp].a(0, [1, T_]), gt.a(0, [1, T_]), ALU.mult, [PS(bp), gtb], [("merged", c)])
                    else:
                        tp, tpb = tmps.next()
                        tt("dve", tp.a(0, [1, T_]), pf[bp].a(0, [1, T_]), gt.a(0, [1, T_]), ALU.mult, [PS(bp), gtb], [tpb])
                        if last:
                            tt("dve", mT.a(c * 512, [1, T_]), mg, tp.a(0, [1, T_]), ALU.add, [("merged", c), tpb], ["mT"])
                        else:
                            tt("dve", mg, mg, tp.a(0, [1, T_]), ALU.add, [("merged", c), tpb], [("merged", c)])


        if cfg.get("nphase", 9) < 2:
            return
        phase()
        Lx = 15 + L
        EXW = NS * Lx
        pext = [al("pext", EXW, F32, bn=("pext", e_)) for e_ in range(8)]
        ta, tb = al("ta", EXW, F32), al("tb", EXW, F32)
        dT = al("dT", 8 * TS, BF16)
        szp = al("szp", 8 * TS, F32)
        ypool = al("ypool", 8 * TS, BF16)
        gts = ring_alloc("gt", TS, F32, 2)
        tmps = ring_alloc("tmpm", TS, F32, 2)
        stages = ring_alloc("tmstage", 512, F32, 1) if need_tm else None
        if prm:
            for e in range(8):
                cp("dve", pext[e].a(0, [1, 15]), phist.a(e * 15, [1, 15]), ["phist"], [AB(("pext", e))])
        else:
            r0, r1 = al("sp_r0", 1024, F32), al("sp_r1", 1024, F32)
            P.dma("sp", r0.a(0), stp_d.a(0, [1, 1024]), writes=[AB("sp_r0")])
            P.dma("sp", r1.a(0, [1, 1024], np_=112), stp_d.a(128 * 1024, [1, 1024], np_=112), writes=[AB("sp_r1")])
            for e in range(8):
                b = trr.next()
                tr(pf[b].a(0, [1, 128]), r0.a(e * 128, [1, 128]), ident_f.a(0), [AB("sp_r0"), "ident_f"], [PS(b)])
                tr(pf[b].a(128, [1, 112]), r1.a(e * 128, [1, 128], np_=112), ident_f.a(0, [1, 112], np_=112),
                   [AB("sp_r1"), "ident_f"], [PS(b)])
                cp("dve", pext[e].a(0, [Lx, 16], [1, 15]), pf[b].a(0, [15, 16], [1, 15]), [PS(b)], [AB(("pext", e))])
            P.dma("sp", bass.AP(pool_s_d.h, 0, [[15 * 1024, 16], [1024, 7], [1, 1024]]),
                  bass.AP(stp_d.h, 8 * 1024, [[15 * 1024, 16], [1024, 7], [1, 1024]]))
        for s2 in range(2):
            sl, slb = load_slab(w_in_d, IN_COLS, 0, OFF_U + s2 * 512)
            for e4 in range(4):
                e = s2 * 4 + e4
                b = fm(sl, slb, e4, T_)
                cp("act", pext[e].a(15, [Lx, NS], [1, L]), pf[b].a(0, [L, NS], [1, L]), [PS(b)], [AB(("pext", e))])
                if prm:
                    cp("dve", phist.a(e * 15, [1, 15]), pext[e].a(L, [1, 15]), [AB(("pext", e))], ["phist"])
            if need_tm:
                def rows_u(stg, stgb, s2=s2):
                    if prm:
                        P.dma("sp", pool_p_d.a(s2 * 512, [1, 512], np_=15), stg.a(0, [1, 512], p0=113, np_=15), reads=[stgb])
                    else:
                        for t8 in range(8):
                            P.dma("sp", bass.AP(pool_s_d.h, (7 + t8) * 1024 + s2 * 512, [[15 * 1024, 16], [1, 512]]),
                                  bass.AP(stg.h, t8 * 512, [[8 * 512, 16], [1, 512]]), reads=[stgb])
                tm_rows(sl, slb, stages, rows_u)
        for e in range(8):
            w = (2, 2, 4, 4, 8, 8, 16, 16)[e]
            x_ = pext[e]
            xb_ = AB(("pext", e))

            def R(t, a0, n):
                return t.a(a0, [Lx, NS], [1, n])
            tt("dve", R(ta, 1, Lx - 1), R(x_, 1, Lx - 1), R(x_, 0, Lx - 1), ALU.add, [xb_], [AB("ta")])
            cur, curb = ta, AB("ta")
            if w >= 4:
                tt("dve", R(tb, 3, Lx - 3), R(ta, 3, Lx - 3), R(ta, 1, Lx - 3), ALU.add, [AB("ta")], [AB("tb")])
                cur, curb = tb, AB("tb")
            if w >= 8:
                tt("dve", R(ta, 7, Lx - 7), R(tb, 7, Lx - 7), R(tb, 3, Lx - 7), ALU.add, [AB("tb")], [AB("ta")])
                cur, curb = ta, AB("ta")
            if w >= 16:
                tt("dve", R(tb, 15, Lx - 15), R(ta, 15, Lx - 15), R(ta, 7, Lx - 15), ALU.add, [AB("ta")], [AB("tb")])
                cur, curb = tb, AB("tb")
            stt("dve", dT.a(e * TS, [L, NS], [1, L]), R(cur, 15, L), 1.0 / w, R(x_, 15, L), ALU.mult, ALU.subtract,
                [curb, xb_], [AB("dT")])
            if prm and ti == 0:
                for t in range(w - 1):
                    stt("dve", dT.a(e * TS + t, [1, 1]), cur.a(15 + t, [1, 1]), 1.0 / (t + 1), x_.a(15 + t, [1, 1]),
                        ALU.mult, ALU.subtract, [curb, xb_, AB("dT")], [AB("dT")])
        for s2 in range(2):
            sl, slb = load_slab(w_in_d, IN_COLS, 0, OFF_ZP + s2 * 512)
            for e4 in range(4):
                e = s2 * 4 + e4
                b = fm(sl, slb, e4, T_)
                act(szp.a(e * TS, [1, T_]), pf[b].a(0, [1, T_]), AF.Silu, [PS(b)], [AB("szp")])
        for g in range(4):
            for half in range(2):
                e = g * 2 + half
                b = mmr.next()
                for kc in range(2):
                    mm(pf[b].a(0, [1, T_]), wgrp.a((g * 2 + kc) * 256 + half * 128, [1, 128]), dT.a((g * 2 + kc) * TS, [1, T_]),
                       kc == 0, kc == 1, ["wgrp", AB("dT")], [PS(b)])
                stt("dve", ypool.a(e * TS, [1, T_]), pf[b].a(0, [1, T_]), pscT.a(e, [1, 1]), szp.a(e * TS, [1, T_]),
                    ALU.mult, ALU.mult, [PS(b), "params", AB("szp")], [AB("ypool")])
        gate_proj(0, ypool, AB("ypool"), 8, w_po_d, True, False, gts, tmps)

        if cfg.get("nphase", 9) < 3:
            return
        phase()
        qT = al("qT", 8 * TS, BF16)
        sza = al("sza", NCH * 1024, F32, parts=[(("sza", j_), j_ * 1024, 1024) for j_ in range(NCH)])
        yaT = al("yaT", 8 * TS, BF16)
        gts = ring_alloc("gt", TS, F32, 2)
        tmps = ring_alloc("tmpm", TS, F32, 2)
        p_ts = ring_alloc("p_t", 1024, BF16, 2)
        pTs = ring_alloc("pT", 1024, BF16, 2)
        yas = ring_alloc("ya", 1024, BF16, 2)
        ast_ = ring_alloc("ast", 16, F32, 2)
        if not prm:
            expb = al("expb", 8 * 16 * 128, BF16)
            Kbs = ring_alloc("Kb", 2048, BF16, 2)
            kTss = ring_alloc("kTs", 2048, BF16, 2)
            Vbs = ring_alloc("Vb", 2048, BF16, 2)
            memset("dve", expb.a(0), 0.0, [AB("expb")])
        for s2 in range(2):
            sl, slb = load_slab(w_in_d, IN_COLS, 0, OFF_Q + s2 * 512)
            for e4 in range(4):
                e = s2 * 4 + e4
                b = fm(sl, slb, e4, T_)
                cp("act", qT.a(e * TS, [1, T_]), pf[b].a(0, [1, T_]), [PS(b)], [AB("qT")])
        zasl = [load_slab(w_in_d, IN_COLS, 0, OFF_ZA + cb * 512) for cb in range(2)]
        zar = Ring([1, 2])

        def zaproj(j):
            for cb in range(2):
                b = zar.next()
                for k in range(8):
                    mm(pf[b].a(0), hT.a(k * 512 + j * 128, [1, 128]), zasl[cb][0].a(k * 512, [1, 512]), k == 0, k == 7,
                       [zasl[cb][1][k], "hT"], [PS(b)])
                act(sza.a(j * 1024 + cb * 512, [1, 512]), pf[b].a(0), AF.Silu, [PS(b)], [AB(("sza", j))])

        def attn(j):
            if prm:
                bs = [5, 6] if j % 2 == 0 else [7, 0]
                for h in range(4):
                    for dc in range(2):
                        i = h * 2 + dc
                        mm(pf[bs[h // 2]].a((h % 2) * 256, [1, 256]), qT.a(i * TS + j * 128, [1, 128]), kT.a(i * 256, [1, 256]),
                           dc == 0, dc == 1, [AB("qT"), "kT"], [PS(bs[h // 2])])
            else:
                bs = [6, 7]
                cp("dve", expb.a(0, [2048, 8], [136, 16], [1, 8]), qT.a(0, [TS, 8], [8, 16], [1, 8]), [AB("qT")], [AB("expb")])
                for bq in range(16):
                    Kb, Kbb = Kbs.next()
                    P.dma("pool", Kb.a(0, [1024, 2], [1, 1024]),
                          bass.AP(ck_d.h, bq * 256 * 1024, [[1024, 128], [128 * 1024, 2], [1, 1024]]), writes=[Kbb])
                    kTs, kTsb = kTss.next()
                    for mc in range(2):
                        bt = trr.next()
                        for i in range(8):
                            tr(pb[bt].a(i * 128, [1, 128]), Kb.a(mc * 1024 + i * 128, [1, 128]), ident_b.a(0), [Kbb, "ident_b"], [PS(bt)])
                        cp("act" if mc == 0 else "dve", kTs.a(mc * 128, [256, 8], [1, 128]), pb[bt].a(0, [128, 8], [1, 128]), [PS(bt)], [kTsb])
                    for h in range(4):
                        for dc in range(2):
                            i = h * 2 + dc
                            mm(pf[bs[h // 2]].a((h % 2) * 256, [1, 256]), expb.a(i * 2048 + bq * 128, [1, 128]), kTs.a(i * 256, [1, 256]),
                               bq == 0 and dc == 0 and h % 2 == 0, bq == 15 and dc == 1, [AB("expb"), kTsb], [PS(bs[h // 2])], skip=True)
            a_, ab_ = ast_.next()
            for hb in range(2):
                P.op("dve", (lambda o_, i_: (lambda e: e.tensor_reduce(out=o_, in_=i_, axis=AX.X, op=ALU.max)))(
                    a_.a(hb * 2, [1, 2]), pf[bs[hb]].a(0, [256, 2], [1, 256])), [PS(bs[hb])], [ab_])
            ts("dve", a_.a(4, [1, 4]), a_.a(0, [1, 4]), -1.0 / 16, None, ALU.mult, None, [ab_], [ab_])
            p_t, p_tb = p_ts.next()
            for h in range(4):
                act(p_t.a(h * 256, [1, 256]), pf[bs[h // 2]].a((h % 2) * 256, [1, 256]), AF.Exp, [PS(bs[h // 2]), ab_], [p_tb, ab_],
                    bias=a_.a(4 + h, [1, 1]), scale=1.0 / 16, accum=a_.a(8 + h, [1, 1]))
            recip(a_.a(12, [1, 4]), a_.a(8, [1, 4]), [ab_], [ab_])
            yield "A"
            bt = trr.next()
            for h in range(4):
                for mc in range(2):
                    tr(pb[bt].a((h * 2 + mc) * 128, [1, 128]), p_t.a(h * 256 + mc * 128, [1, 128]), ident_b.a(0), [p_tb, "ident_b"], [PS(bt)])
            pT, pTb = pTs.next()
            cp("act", pT.a(0), pb[bt].a(0), [PS(bt)], [pTb])
            if prm:
                bo = bs
                for h in range(4):
                    for mc in range(2):
                        mm(pf[bo[h // 2]].a((h % 2) * 256, [1, 256]), pT.a((h * 2 + mc) * 128, [1, 128]), v_b.a(mc * 1024 + h * 256, [1, 256]),
                           mc == 0, mc == 1, [pTb, "v_b"], [PS(bo[h // 2])])
            else:
                bo = [6, 7]
                cp("dve", expb.a(0, [2048, 8], [136, 16], [1, 8]), pT.a(0, [128, 8], [8, 16], [1, 8]), [pTb], [AB("expb")])
                for bq in range(16):
                    Vb, Vbb = Vbs.next()
                    P.dma("pool", Vb.a(0, [1024, 2], [1, 1024]),
                          bass.AP(cv_d.h, bq * 256 * 1024, [[1024, 128], [128 * 1024, 2], [1, 1024]]), writes=[Vbb])
                    for h in range(4):
                        for mc in range(2):
                            mm(pf[bo[h // 2]].a((h % 2) * 256, [1, 256]), expb.a((h * 2 + mc) * 2048 + bq * 128, [1, 128]),
                               Vb.a(mc * 1024 + h * 256, [1, 256]), bq == 0 and mc == 0 and h % 2 == 0, bq == 15 and mc == 1,
                               [AB("expb"), Vbb], [PS(bo[h // 2])], skip=True)
            ya, yab = yas.next()
            for h in range(4):
                stt("dve", ya.a(h * 256, [1, 256]), pf[bo[h // 2]].a((h % 2) * 256, [1, 256]), a_.a(12 + h, [1, 1]),
                    sza.a(j * 1024 + h * 256, [1, 256]), ALU.mult, ALU.mult, [PS(bo[h // 2]), ab_, AB(("sza", j))], [yab])
            yield "B"
            bt = trr.next()
            for c in range(8):
                tr(pb[bt].a(c * 128, [1, 128]), ya.a(c * 128, [1, 128]), ident_b.a(0), [yab, "ident_b"], [PS(bt)])
            cp("act", yaT.a(j * 128, [TS, 8], [1, 128]), pb[bt].a(0, [128, 8], [1, 128]), [PS(bt)], [AB("yaT")])

        def run_to(g_, marks):
            while True:
                try:
                    r = next(g_)
                except StopIteration:
                    return None
                if r in marks:
                    return r

        agens = [attn(j) for j in range(NCH)]
        if not prm:
            zaproj(0)
        run_to(agens[0], ("A",))
        for j in range(NCH):
            if prm:
                zaproj(j)
            if j + 1 < NCH:
                run_to(agens[j + 1], ("A",))
            run_to(agens[j], ("B",))
            run_to(agens[j], ())
        gate_proj(2, yaT, AB("yaT"), 8, w_ao_d, False, False, gts, tmps)

        if cfg.get("nphase", 9) < 4:
            return
        phase()
        W3 = 3 + L
        dtr = al("dtr", NCH * 32, F32)
        dtv = al("dtv", NCH * 32, F32)
        sp1 = al("sp1", NCH * 32, F32)
        av = al("av", NCH * 32, F32)
        xs_tm = al("xs_tm", NCH * 2048, BF16)
        B_tm = al("B_tm", NCH * 512, BF16)
        BT = al("BT", 4 * TS, BF16)
        CT = al("CT", 4 * TS, BF16)
        ysT = al("ysT", 16 * TS, BF16)
        gts = ring_alloc("gt", TS, F32, 2)
        tmps = ring_alloc("tmpm", TS, F32, 2)
        SUB0 = sb.mark()
        cexts = ring_alloc("cext", NS * W3, F32, 4)
        caccs = ring_alloc("cacc", TS, F32, 3)
        xcs = ring_alloc("xc", TS, BF16, 4)
        stages = ring_alloc("tmstage", 512, F32, 1) if need_tm else None
        if not prm:
            chs = al("chs", 24 * 48, F32)
            stc_sb = al("stc_sb", 3072, F32)
            P.dma("sp", stc_sb.a(0, [1, 3072], np_=48), stc_d.a(0, [1, 3072], np_=48), writes=[AB("stc_sb")])
            for e8 in range(3):
                b = trr.next()
                for i in range(8):
                    e = e8 * 8 + i
                    tr(pf[b].a(i * 48, [1, 48]), stc_sb.a(e * 128, [1, 128], np_=48), ident_f.a(0, [1, 48], np_=48),
                       [AB("stc_sb"), "ident_f"], [PS(b)])
                cp("dve", chs.a(e8 * 8 * 48, [1, 384]), pf[b].a(0, [1, 384]), [PS(b)], [AB("chs")])
        sl, slb = load_slab(w_in_d, IN_COLS, 0, OFF_DT, 8, 32)
        for j in range(NCH):
            b = tmm(sl, slb, j, 32)
            tt("dve", dtr.a(j * 32, [1, 32]), pf[b].a(0, [1, 32]), dtb_bc.a(0), ALU.add, [PS(b), "params"], [AB("dtr")])
        stt("dve", sp1.a(0), dtr.a(0), -1.0, dtr.a(0), ALU.mult, ALU.max, [AB("dtr")], [AB("sp1")])
        act(sp1.a(0), sp1.a(0), AF.Exp, [AB("sp1")], [AB("sp1")], scale=-1.0)
        act(sp1.a(0), sp1.a(0), AF.Ln, [AB("sp1")], [AB("sp1")], bias=1.0)
        stt("dve", dtv.a(0), dtr.a(0), 0.0, sp1.a(0), ALU.max, ALU.add, [AB("dtr"), AB("sp1")], [AB("dtv")])
        tt("dve", av.a(0, [32, NCH], [1, 32]), dtv.a(0, [32, NCH], [1, 32]), A_bc.a(0, [0, NCH], [1, 32]), ALU.mult,
           [AB("dtv"), "params"], [AB("av")])
        pendB = []

        def convB(e, xc, xcb):
            if e < 16:
                bt = trr.next()
                for jj in range(NCH):
                    tr(pb[bt].a(jj * 128, [1, 128]), xc.a(jj * 128, [1, 128]), ident_b.a(0), [xcb, "ident_b"], [PS(bt)])
                cp("act", xs_tm.a(e * 128, [2048, NCH], [1, 128]), pb[bt].a(0, [128, NCH], [1, 128]), [PS(bt)], [AB("xs_tm")])
            elif e < 20:
                g = e - 16
                bt = trr.next()
                for jj in range(NCH):
                    tr(pb[bt].a(jj * 128, [1, 128]), BT.a(g * TS + jj * 128, [1, 128]), ident_b.a(0), [AB("BT"), "ident_b"], [PS(bt)])
                cp("act", B_tm.a(g * 128, [512, NCH], [1, 128]), pb[bt].a(0, [128, NCH], [1, 128]), [PS(bt)], [AB("B_tm")])

        for s6 in range(6):
            sl, slb = load_slab(w_in_d, IN_COLS, 0, OFF_XBC + s6 * 512)
            for e4 in range(4):
                e = s6 * 4 + e4
                b = fm(sl, slb, e4, T_)
                ce, ceb = cexts.next()
                if prm:
                    cp("dve", ce.a(0, [1, 3]), chist.a(e * 3, [1, 3]), ["chist"], [ceb])
                else:
                    cp("dve", ce.a(0, [W3, 16], [1, 3]), chs.a(e * 48, [3, 16], [1, 3]), [AB("chs")], [ceb])
                cp("act", ce.a(3, [W3, NS], [1, L]), pf[b].a(0, [L, NS], [1, L]), [PS(b), ceb], [ceb])
                if prm:
                    cp("dve", chist.a(e * 3, [1, 3]), ce.a(L, [1, 3]), [ceb], ["chist"])
                ca, cab = caccs.next()
                cav = ca.a(0, [L, NS], [1, L])
                act(cav, ce.a(0, [W3, NS], [1, L]), AF.Identity, [ceb, "params"], [cab], bias=cbT.a(e, [1, 1]), scale=cwT.a(e * 4, [1, 1]))
                for k in range(1, 4):
                    stt("dve", cav, ce.a(k, [W3, NS], [1, L]), cwT.a(e * 4 + k, [1, 1]), cav, ALU.mult, ALU.add, [ceb, cab, "params"], [cab])
                if e < 16:
                    xc, xcb = xcs.next()
                    act(xc.a(0, [1, T_]), ca.a(0, [1, T_]), AF.Silu, [cab], [xcb])
                    pendB.append((e, xc, xcb))
                elif e < 20:
                    g = e - 16
                    act(BT.a(g * TS, [1, T_]), ca.a(0, [1, T_]), AF.Silu, [cab], [AB("BT")])
                    pendB.append((e, None, None))
                else:
                    g = e - 20
                    act(CT.a(g * TS, [1, T_]), ca.a(0, [1, T_]), AF.Silu, [cab], [AB("CT")])
                if len(pendB) > 3:
                    convB(*pendB.pop(0))
            if need_tm:
                def rows_c(stg, stgb, s6=s6):
                    if prm:
                        P.dma("sp", conv_p_d.a(s6 * 512, [1, 512], np_=3), stg.a(0, [1, 512], p0=125, np_=3), reads=[stgb])
                    else:
                        for j3 in range(3):
                            P.dma("sp", bass.AP(conv_s_d.h, j3 * 3072 + s6 * 512, [[3 * 3072, 16], [1, 512]]),
                                  bass.AP(stg.h, (5 + j3) * 512, [[8 * 512, 16], [1, 512]]), reads=[stgb])
                tm_rows(sl, slb, stages, rows_c)
        while pendB:
            convB(*pendB.pop(0))
        sb.reset(SUB0)
        smalls = ring_alloc("ssd_small", 32 * 8, F32, 2)
        acsTs = ring_alloc("acsT", 128, F32, 2)
        abfs = ring_alloc("abf", 256, BF16, 2)
        xdts = ring_alloc("xdt", 2048, BF16, 2 if prm else 1)
        xdds = ring_alloc("xdd", 2048, BF16, 2 if prm else 1)
        CBms = ring_alloc("CBm", 512, F32, 1)
        szss = ring_alloc("szs", 2048, F32, 1)
        Es = ring_alloc("E", 512, F32, 2)
        MTs = ring_alloc("MT", 512, BF16, 3)
        t1s = ring_alloc("t1", 512, F32, 2)
        t2s = ring_alloc("t2", 512, F32, 1)
        ys = ring_alloc("yv", 2048, F32, 1)
        yns = ring_alloc("yn", 2048, BF16, 1)
        zsl = [load_slab(w_in_d, IN_COLS, 0, OFF_ZS + cb * 512) for cb in range(4)]
        tri = tri_p if prm else tri_s
        trib = "tri_p" if prm else "tri_s"
        maskb, maskbn = (maskb_p, "maskb_p") if prm else (maskb_s, "maskb_s")
        Smat, Smb = (ones_f, "ones_f") if prm else (S_s, "S_s")

        def chunk(j):
            sm_, smb = smalls.next()
            acs, nacs, eacs, dlt, dec, etot, dtd, cdv = [sm_.a(i * 32, [1, 32]) for i in range(8)]
            a_j = av.a(j * 32, [1, 32])
            dt_j = dtv.a(j * 32, [1, 32])
            b1 = trr.next()
            mm(pf[b1].a(0, [1, 32]), tri.a(0), a_j, True, True, [trib, AB("av")], [PS(b1)])
            mm(pf[b1].a(32, [1, 32]), Smat.a(0), a_j, True, True, [Smb, AB("av")], [PS(b1)])
            mm(pf[b1].a(64, [1, 128], np_=32), a_j, tri.a(0), True, True, [trib, AB("av")], [PS(b1)])
            cp("dve", acs, pf[b1].a(0, [1, 32]), [PS(b1)], [smb])
            ts("dve", nacs, pf[b1].a(0, [1, 32]), -1.0, None, ALU.mult, None, [PS(b1)], [smb])
            act(eacs, pf[b1].a(0, [1, 32]), AF.Exp, [PS(b1)], [smb])
            tt("dve", dlt, pf[b1].a(32, [1, 32]), acs, ALU.subtract, [PS(b1), smb], [smb])
            act(dec, dlt, AF.Exp, [smb], [smb])
            act(etot, pf[b1].a(32, [1, 32]), AF.Exp, [PS(b1)], [smb])
            acsT, acsTb = acsTs.next()
            cp("act", acsT.a(0, [1, 128], np_=32), pf[b1].a(64, [1, 128], np_=32), [PS(b1)], [acsTb])
            abf, abfb = abfs.next()
            cp("dve", abf.a(0, [1, 128], np_=32), acsT.a(0, [1, 128], np_=32), [acsTb], [abfb])
            tt("dve", abf.a(128, [1, 128], np_=32), acsT.a(0, [1, 128], np_=32), abf.a(0, [1, 128], np_=32), ALU.subtract, [acsTb, abfb], [abfb])
            tt("dve", dtd, dt_j, dec, ALU.mult, [AB("dtv"), smb], [smb])
            xdt, xdtb = xdts.next()
            xdd, xddb = xdds.next()
            tt(cfg.get("pool_eng", "dve"), xdt.a(0, [64, 32], [1, 64]), xs_tm.a(j * 2048, [64, 32], [1, 64]), dtv.a(j * 32, [1, 32], [0, 64]), ALU.mult,
               [AB("xs_tm"), AB("dtv")], [xdtb])
            tt(cfg.get("pool_eng", "dve"), xdd.a(0, [64, 32], [1, 64]), xs_tm.a(j * 2048, [64, 32], [1, 64]), sm_.a(6 * 32, [1, 32], [0, 64]), ALU.mult,
               [AB("xs_tm"), smb], [xddb])
            bc = auxr.next()
            for g in range(4):
                mm(pf[bc].a(g * 128, [1, 128]), BT.a(g * TS + j * 128, [1, 128]), CT.a(g * TS + j * 128, [1, 128]), True, True,
                   [AB("BT"), AB("CT")], [PS(bc)])
            CBm, CBmb = CBms.next()
            tt("dve", CBm.a(0, [128, 4], [1, 128]), pf[bc].a(0, [128, 4], [1, 128]), tri.a(0, [0, 4], [1, 128]), ALU.mult,
               [PS(bc), trib], [CBmb])
            szs, szsb = szss.next()
            for cb in range(4):
                b = tmm(zsl[cb][0], zsl[cb][1], j)
                act(szs.a(cb * 512, [1, 512]), pf[b].a(0), AF.Silu, [PS(b)], [szsb])
            yield "HB"
            if not prm:
                etT = al("etT", 128, F32)
                cdN = al("cdN", 256, F32)
                CTbs = ring_alloc("CTb", 512, BF16, 2)
                h0ns = ring_alloc("h0n", 2048, F32, 3)
                h0bs = ring_alloc("h0b", 2048, BF16, 2)
                h0Ts = ring_alloc("h0T", 2048, BF16, 2)
                xddbs = ring_alloc("xdd_b", 2048, BF16, 2)
                mR = sb.mark()
                Rrep = al("Rrep", 2048, F32)
                memset("pool", Rrep.a(0, np_=32), 0.0, [AB("Rrep")])
                asel(Rrep.a(0, [128, 16], [64, 2], [1, 64], np_=32), Rrep.a(0, [128, 16], [64, 2], [1, 64], np_=32),
                     [[-2, 16], [-1, 2], [0, 64]], ALU.not_equal, 1.0, 0, 1, [AB("Rrep")], [AB("Rrep")])
                for CTb, CTbb in CTbs.items:
                    memset("pool", CTb.a(0), 0.0, [CTbb])
                bq_ = trr.next()
                tr(pf[bq_].a(0, [1, 128], np_=32), sm_.a(5 * 32, [1, 32]), ident_f.a(0), [smb, "ident_f"], [PS(bq_)])
                cp("dve", etT.a(0, [1, 128], np_=32), pf[bq_].a(0, [1, 128], np_=32), [PS(bq_)], [AB("etT")])
                bq_ = trr.next()
                for rc in range(16):
                    mm(pf[bq_].a(rc * 16, [1, 16]), Rrep.a(rc * 128, [1, 128], np_=32), etT.a(7, [8, 16], np_=32), True, True,
                       [AB("Rrep"), AB("etT")], [PS(bq_)])
                cp("dve", cdN.a(0), pf[bq_].a(0, [1, 256]), [PS(bq_)], [AB("cdN")])
                sb.reset(mR)
                hfs = ring_alloc("hf", 2048, F32, 2)
                yoff_banks = [0, 1, 2, 5]
                st_ring = Ring([6, 7])

                def make_xq(b_):
                    xq_, xqb_ = xddbs.next()
                    ts("dve", xq_.a(0), xdd.a(0), seqmask.a(b_, [1, 1]), None, ALU.mult, None, [xddb, "seqmask"], [xqb_])
                    return xq_, xqb_
                xq_next = make_xq(0)
                def seq_s1(bq):
                    h0n, h0nb = h0ns.next()
                    P.dma("sp", h0n.a(0, [128, 16], [1, 128]),
                          bass.AP(sts_d.h, bq * 2048 * 128, [[128, 128], [128 * 128, 16], [1, 128]]), writes=[h0nb])
                    h0b, h0bb = h0bs.next()
                    cp("act", h0b.a(0), h0n.a(0), [h0nb], [h0bb])
                    h0T, h0Tb = h0Ts.next()
                    for half in range(2):
                        bt = trr.next()
                        for i in range(8):
                            tr(pb[bt].a(i * 128, [1, 128]), h0b.a((half * 8 + i) * 128, [1, 128]), ident_b.a(0), [h0bb, "ident_b"], [PS(bt)])
                        cp("act", h0T.a(half * 1024, [1, 1024]), pb[bt].a(0), [PS(bt)], [h0Tb])
                    CTb, CTbb = CTbs.next()
                    cp("dve", CTb.a(bq * 8, [128, 4], [1, 8]), CT.a(bq * 8, [TS, 4], [1, 8]), [AB("CT")], [CTbb])
                    for g in range(4):
                        mm(pf[yoff_banks[g]].a(0), CTb.a(g * 128, [1, 128]), h0T.a(g * 512, [1, 512]), bq == 0, bq == 15,
                           [CTbb, h0Tb], [PS(yoff_banks[g])])
                    memset("dve", CTb.a(bq * 8, [128, 4], [1, 8]), 0.0, [CTbb])
                    return h0n, h0nb

                def seq_s2(bq, h0n, h0nb, xq, xqb):
                    hf, hfb = hfs.next()
                    for g in range(4):
                        bs_ = st_ring.next()
                        for r4 in range(4):
                            rc = g * 4 + r4
                            mm(pf[bs_].a(r4 * 128, [1, 128]), xq.a(rc * 128, [1, 128]), B_tm.a(g * 128, [1, 128]), True, True,
                               [xqb, AB("B_tm")], [PS(bs_)])
                        for r4 in range(4):
                            rc = g * 4 + r4
                            stt("dve", hf.a(rc * 128, [1, 128]), h0n.a(rc * 128, [1, 128]), cdN.a(rc * 16 + bq, [1, 1]),
                                pf[bs_].a(r4 * 128, [1, 128]), ALU.mult, ALU.add, [h0nb, AB("cdN"), PS(bs_)], [hfb])
                    P.dma("pool", bass.AP(ssm_s_d.h, bq * 2048 * 128, [[128, 128], [128 * 128, 16], [1, 128]]),
                          hf.a(0, [128, 16], [1, 128]), reads=[hfb])

                pend1 = seq_s1(0)
                for bq in range(16):
                    nxt1 = seq_s1(bq + 1) if bq < 15 else None
                    xq, xqb = xq_next
                    if bq < 15:
                        xq_next = make_xq(bq + 1)
                    seq_s2(bq, pend1[0], pend1[1], xq, xqb)
                    pend1 = nxt1
            yv, yvb = ys.next()
            POOLE = cfg.get("pool_eng", "dve")

            def stage1(hb):
                g = hb // 2
                bb = trr.next()
                for hh in range(4):
                    h = hb * 4 + hh
                    o_ = pf[bb].a(hh * 128, [1, 128])
                    for part in range(2):
                        mm(o_, ident_b.a(h, [0, 128], np_=32), abf.a(part * 128, [1, 128], np_=32), hh == 0 and part == 0, False,
                           ["ident_b", abfb], [PS(bb)], skip=True)
                mm(pf[bb].a(0, [1, 512]), ident_b.a(0), maskb.a(0, [0, 4], [1, 128]), False, True, ["ident_b", maskbn], [PS(bb)], skip=True)
                E, Eb = Es.next()
                for hh in range(4):
                    h = hb * 4 + hh
                    act(E.a(hh * 128, [1, 128]), pf[bb].a(hh * 128, [1, 128]), AF.Exp, [PS(bb), smb], [Eb], bias=sm_.a(32 + h, [1, 1]))
                MT, MTb = MTs.next()
                stt("dve", MT.a(0, [128, 4], [1, 128]), E.a(0, [128, 4], [1, 128]), 1.0, CBm.a(g * 128, [0, 4], [1, 128]),
                    ALU.min, ALU.mult, [Eb, CBmb], [MTb])
                return MT, MTb

            def stage2(hb, MT, MTb, byd):
                for hh in range(4):
                    h = hb * 4 + hh
                    mm(pf[byd].a((h % 8) * 64, [1, 64]), MT.a(hh * 128, [1, 128]), xdt.a(h * 64, [1, 64]), True, True,
                       [MTb, xdtb], [PS(byd)])

            pend = stage1(0)
            byd = None
            for hb in range(8):
                g = hb // 2
                nxt = stage1(hb + 1) if hb < 7 else None
                if hb % 2 == 0:
                    byd = auxr.next() if prm else st_ring.next()
                stage2(hb, pend[0], pend[1], byd)
                pend = nxt
                if hb % 2 == 0:
                    continue
                if hb == 3:
                    yield "B1"
                if prm:
                    byo = auxr.next()
                    mm(pf[byo].a(0), CT.a(g * TS + j * 128, [1, 128]), hst_b.a(g * 512, [1, 512]), True, True, [AB("CT"), "hst_b"], [PS(byo)])
                else:
                    byo = yoff_banks[g]
                t1, t1b = t1s.next()
                t2, t2b = t2s.next()
                tt("dve", t1.a(0, [64, 8], [1, 64]), pf[byo].a(0, [64, 8], [1, 64]), sm_.a(2 * 32 + g * 8, [1, 8], [0, 64]), ALU.mult,
                   [PS(byo), smb], [t1b])
                tt("dve", t1.a(0), pf[byd].a(0), t1.a(0), ALU.add, [PS(byd), t1b], [t1b])
                tt(POOLE, t2.a(0, [64, 8], [1, 64]), xs_tm.a(j * 2048 + g * 512, [64, 8], [1, 64]), D_bc.a(g * 8, [1, 8], [0, 64]), ALU.mult,
                   [AB("xs_tm"), "params"], [t2b])
                tt("dve", t1.a(0), t1.a(0), t2.a(0), ALU.add, [t1b, t2b], [t1b])
                tt("dve", yv.a(g * 512, [1, 512]), t1.a(0), szs.a(g * 512, [1, 512]), ALU.mult, [t1b, szsb], [yvb])
            yield "BT"
            if prm:
                bcd = trr.next()
                mm(pf[bcd].a(0, [1, 32]), ident_f.a(127, [0, 128]), etot, True, True, ["ident_f", smb], [PS(bcd)])
                cp("dve", cdv, pf[bcd].a(0, [1, 32]), [PS(bcd)], [smb])
                for g in range(4):
                    bs_ = auxr.next()
                    mm(pf[bs_].a(0), B_tm.a(j * 512 + g * 128, [1, 128]), xdd.a(g * 512, [1, 512]), True, True, [AB("B_tm"), xddb], [PS(bs_)])
                    t2, t2b = t2s.next()
                    tt(POOLE, t2.a(0, [64, 8], [1, 64]), hst.a(g * 512, [64, 8], [1, 64]), sm_.a(7 * 32 + g * 8, [1, 8], [0, 64]), ALU.mult,
                       ["hst", smb], [t2b])
                    tt("dve", hst.a(g * 512, [1, 512]), t2.a(0), pf[bs_].a(0), ALU.add, [t2b, PS(bs_)], ["hst"])
                    cp("act", hst_b.a(g * 512, [1, 512]), hst.a(g * 512, [1, 512]), ["hst"], ["hst_b"])
            r_, rb_ = rstd_of(yv.a(0), yvb, 2048)
            yn, ynb = yns.next()
            ts(POOLE, yn.a(0), yv.a(0), r_, None, ALU.mult, None, [yvb, rb_], [ynb])
            yield "TA"
            for half in range(2):
                bt = trr.next()
                for c8 in range(8):
                    tr(pb[bt].a(c8 * 128, [1, 128]), yn.a((half * 8 + c8) * 128, [1, 128]), ident_b.a(0), [ynb, "ident_b"], [PS(bt)])
                tt("dve", ysT.a(half * 8 * TS + j * 128, [TS, 8], [1, 128]), pb[bt].a(0, [128, 8], [1, 128]),
                   snwT.a(half * 8, [1, 8], [0, 128]), ALU.mult, [PS(bt), "params"], [AB("ysT")])
        def run_until(g_, marks):
            while True:
                try:
                    r = next(g_)
                except StopIteration:
                    return None
                if r in marks:
                    return r

        gens = [chunk(j) for j in range(NCH)]
        run_until(gens[0], ("HB",))
        for j in range(NCH):
            run_until(gens[j], ("B1",))
            if j > 0:
                run_until(gens[j - 1], ())
            run_until(gens[j], ("BT",))
            if j + 1 < NCH:
                run_until(gens[j + 1], ("HB",))
            run_until(gens[j], ("TA",))
        run_until(gens[NCH - 1], ())
        if nxt_tile is not None and cfg.get("prenorm", True):
            phaseA_norm(*nxt_tile)
        gate_proj(1, ysT, AB("ysT"), 16, w_so_d, False, True, gts, tmps)

        if cfg.get("nphase", 9) < 5:
            return
        phase()
        wo = [load_slab(w_o_d, 1024, 0, cb * 512) for cb in range(2)]
        if last_p:
            hfin = al("hfin", 2048, F32)
            for q4 in range(4):
                bt = trr.next()
                for i in range(4):
                    rc = q4 * 4 + i
                    tr(pf[bt].a(i * 128, [1, 128]), hst.a(rc * 128, [1, 128]), ident_f.a(0), ["hst", "ident_f"], [PS(bt)])
                cp("act", hfin.a(q4 * 512, [1, 512]), pf[bt].a(0), [PS(bt)], [AB("hfin")])
            P.dma("pool", bass.AP(ssm_p_d.h, 0, [[128, 128], [128 * 128, 16], [1, 128]]), hfin.a(0, [128, 16], [1, 128]), reads=[AB("hfin")])
        fnw_bc = al("fnw_bc", 1024, F32)
        P.dma("sp", fnw_bc.a(0), bass.AP(fnw_d.h, 0, [[0, 128], [1, 1024]]), writes=[AB("fnw_bc")])
        xrs = ring_alloc("xr", 1024, F32, 4)
        xos = ring_alloc("xo", 1024, F32, 2)
        yos = ring_alloc("yo", 1024, F32, 2)
        for j in range(NCH):
            xr, xrb = xrs.next()
            P.dma("sp", xr.a(0), x_d.a((row0 + j * 128) * 1024, [1, 1024]), writes=[xrb])
            xo, xob = xos.next()
            for cb in range(2):
                b = mmr.next()
                for k in range(8):
                    mm(pf[b].a(0), mT.a(k * 512 + j * 128, [1, 128]), wo[cb][0].a(k * 512, [1, 512]), k == 0, k == 7, ["mT", wo[cb][1][k]], [PS(b)])
                tt("dve", xo.a(cb * 512, [1, 512]), pf[b].a(0), xr.a(cb * 512, [1, 512]), ALU.add, [PS(b), xrb], [xob])
            r_, rb_ = rstd_of(xo.a(0), xob, 1024)
            yo, yob = yos.next()
            stt("dve", yo.a(0), xo.a(0), r_, fnw_bc.a(0), ALU.mult, ALU.mult, [xob, rb_, AB("fnw_bc")], [yob])
            P.dma("pool", y_d.a((row0 + j * 128) * 1024, [1, 1024]), yo.a(0), reads=[yob])

    for i_t, (kind, ti) in enumerate(tiles):
        tile(kind, ti, tiles[i_t + 1] if i_t + 1 < len(tiles) else None)
    stats = P.emit()
    if cfg.get("verbose"):
        print("ops/waits per engine:", stats, "sbuf peak", sb.peak)
    return nc, dbg_outs


def _fm(v, nchunk):
    return np.ascontiguousarray(np.asarray(v, np.float32).reshape(nchunk, 128).T)


def make_in_maps(inputs, ncores=8):
    f = lambda a: np.ascontiguousarray(np.asarray(a, np.float32))
    conv_w = f(inputs["conv_w"])[0]
    cwT = np.ascontiguousarray(conv_w.reshape(4, 24, 128).transpose(2, 1, 0).reshape(128, 96))
    shared = {
        "w_in": f(inputs["w_in"])[0], "w_grp": f(inputs["w_pool_grp"])[0].reshape(1024, 256),
        "w_mk": f(inputs["w_mem_k"])[0], "w_mv": f(inputs["w_mem_v"])[0],
        "w_po": f(inputs["w_pool_out"])[0], "w_so": f(inputs["w_ssd_out"])[0],
        "w_ao": f(inputs["w_att_out"])[0], "w_o": f(inputs["w_out"])[0],
        "nwT": _fm(inputs["norm_w"][0], 8), "mnwT": _fm(inputs["mem_norm_w"][0], 8), "pscT": _fm(inputs["pool_scale"][0], 8),
        "cwT": cwT, "cbT": _fm(inputs["conv_b"][0], 24), "snwT": _fm(inputs["ssd_norm_w"][0], 16),
        "fnw": f(inputs["final_norm_w"]).reshape(1, 1024), "dtb": f(inputs["dt_bias"]).reshape(1, 32),
        "alog": f(inputs["a_log"]).reshape(1, 32), "dsk": f(inputs["d_skip"]).reshape(1, 32),
    }
    maps = []
    for c in range(ncores):
        s = slice(16 * c, 16 * c + 16)
        m = dict(shared)
        m.update({
            "xp": f(inputs["x_prompt"][c]), "xs": f(inputs["x_sample"][s]).reshape(128, 1024),
            "mem": f(inputs["mem_prompt"][c]),
            "stp": f(inputs["state_pool"][0, s]).reshape(240, 1024), "stc": f(inputs["state_conv"][0, s]).reshape(48, 3072),
            "sts": f(inputs["state_ssm"][0, s]).reshape(32768, 128),
            "ck": f(inputs["cache_mem_k"][0, s]).reshape(4096, 1024), "cv": f(inputs["cache_mem_v"][0, s]).reshape(4096, 1024),
        })
        maps.append(m)
    return maps


_NC_CACHE = {}


def kernel(**inputs):
    if "nc" not in _NC_CACHE:
        _NC_CACHE["nc"] = build()[0]
    nc = _NC_CACHE["nc"]
    maps = make_in_maps(inputs)
    res = run_bass_kernel_spmd(nc, maps, core_ids=list(range(8)))
    R = res.results
    cat = lambda k: np.stack([np.asarray(r[k], np.float32) for r in R])
    y_p = cat("y_p")
    y_s = cat("y_s").reshape(128, 8, 1024)
    pool_p = cat("pool_p")[None]
    conv_p = cat("conv_p")[None]
    ssm_p = cat("ssm_p").reshape(1, 8, 4, 8, 64, 128)
    mk_p = cat("mk_p").reshape(1, 8, 256, 4, 256)
    mv_p = cat("mv_p").reshape(1, 8, 256, 4, 256)
    pool_s = cat("pool_s").reshape(1, 128, 15, 1024)
    conv_s = cat("conv_s").reshape(1, 128, 3, 3072)
    ssm_s = cat("ssm_s").reshape(1, 128, 4, 8, 64, 128)
    return (y_p, y_s, pool_p, conv_p, ssm_p, mk_p, mv_p, pool_s, conv_s, ssm_s)
```
